# Optimizing a Trainium2 kernel written in Bass

```python
import jax, jax.numpy as jnp
from jax import lax
import numpy as np

D_MODEL = 2048
BATCH = 8
SEQ = 2048
DEPTH = 2

PLE_DIM = 256
MIX_WIDTH = D_MODEL
DN_WIDTH = MIX_WIDTH // 2
HG_WIDTH = MIX_WIDTH - DN_WIDTH
DN_HEAD_DIM = 128
DN_HEADS = DN_WIDTH // DN_HEAD_DIM
HG_HEAD_DIM = 128
HG_HEADS = HG_WIDTH // HG_HEAD_DIM
CONV_WIDTH = 4
DN_CHUNK = 64
HG_CHUNK = 64
NORM_EPS = 1e-6
L2_EPS = 1e-6
SPLITS = (3 * DN_WIDTH, DN_WIDTH, DN_HEADS, DN_HEADS, HG_WIDTH, HG_WIDTH, HG_WIDTH, HG_WIDTH)
IN_WIDTH = sum(SPLITS)

kernel_name = "hybrid_gdn_hgrn2_parallel_heads"


def rms_norm(x, w):
    xf = x.astype(jnp.float32)
    y = xf * lax.rsqrt(jnp.mean(xf * xf, axis=-1, keepdims=True) + NORM_EPS)
    return (y * w.astype(jnp.float32)).astype(x.dtype)


def l2_norm(x):
    return x * lax.rsqrt(jnp.sum(x * x, axis=-1, keepdims=True) + L2_EPS)


def masked_exp(mask, diff):
    return jnp.where(mask, jnp.exp(jnp.where(mask, diff, 0.0)), 0.0)


def to_heads(t, n_heads):
    B, S, W = t.shape
    return t.reshape(B, S, n_heads, W // n_heads).transpose(0, 2, 1, 3)


def gated_head_norm(o, z, w):
    B, H, S, d = o.shape
    o = o.transpose(0, 2, 1, 3)
    o = o * lax.rsqrt(jnp.mean(o * o, axis=-1, keepdims=True) + NORM_EPS) * w.astype(jnp.float32)
    o = o * jax.nn.silu(z.astype(jnp.float32).reshape(B, S, H, d))
    return o.reshape(B, S, H * d)


def causal_conv(x, w):
    C = x.shape[-1]
    return lax.conv_general_dilated(
        x, w[:, None, :].astype(x.dtype), window_strides=(1,),
        padding=[(CONV_WIDTH - 1, 0)], dimension_numbers=("NWC", "WIO", "NWC"),
        feature_group_count=C)


def chunk_gated_delta_rule(q, k, v, g, beta):
    B, H, S, dk = q.shape
    dv = v.shape[-1]
    C = DN_CHUNK
    N = S // C
    q, k, v = (t.reshape(B, H, N, C, t.shape[-1]) for t in (q, k, v))
    g = g.reshape(B, H, N, C)
    beta = beta.reshape(B, H, N, C)
    G = jnp.cumsum(g, axis=-1)
    causal = jnp.tril(jnp.ones((C, C), bool))
    strict = jnp.tril(jnp.ones((C, C), bool), -1)
    decay = masked_exp(causal, G[..., :, None] - G[..., None, :])
    k_beta = k * beta[..., None]
    A = jnp.where(strict, jnp.einsum("bhncd,bhnsd->bhncs", k_beta, k) * decay, 0.0)
    eye = jnp.eye(C, dtype=A.dtype)
    rhs = jnp.concatenate([v * beta[..., None], k_beta * jnp.exp(G)[..., None]], axis=-1)
    sol = lax.linalg.triangular_solve(A + eye, rhs, left_side=True, lower=True,
                                      unit_diagonal=True)
    u, w = sol[..., :dv], sol[..., dv:]
    qk = jnp.einsum("bhncd,bhnsd->bhncs", q, k) * decay
    q_decay = q * jnp.exp(G)[..., None]
    k_tail = k * jnp.exp(G[..., -1:] - G)[..., None]
    tail = jnp.exp(G[..., -1])

    def step(state, xs):
        u_c, w_c, qk_c, qd_c, kt_c, tail_c = xs
        v_new = u_c - jnp.einsum("bhcd,bhde->bhce", w_c, state)
        o = (jnp.einsum("bhcd,bhde->bhce", qd_c, state)
             + jnp.einsum("bhcs,bhse->bhce", qk_c, v_new))
        state = tail_c[..., None, None] * state + jnp.einsum("bhcd,bhce->bhde", kt_c, v_new)
        return state, o

    xs = tuple(jnp.moveaxis(t, 2, 0) for t in (u, w, qk, q_decay, k_tail, tail))
    _, o = lax.scan(step, jnp.zeros((B, H, dk, dv), q.dtype), xs)
    return jnp.moveaxis(o, 0, 2).reshape(B, H, S, dv)


def chunk_hgrn2(q, k, v, log_f):
    B, H, S, dk = q.shape
    dv = v.shape[-1]
    C = HG_CHUNK
    N = S // C
    q, k, v, log_f = (t.reshape(B, H, N, C, t.shape[-1]) for t in (q, k, v, log_f))
    G = jnp.cumsum(log_f, axis=3)
    q_decay = q * jnp.exp(G)
    k_tail = k * jnp.exp(G[:, :, :, -1:] - G)
    tail = jnp.exp(G[:, :, :, -1])
    causal = jnp.tril(jnp.ones((C, C), bool))[:, :, None]

    def step(state, xs):
        q_c, k_c, v_c, G_c, qd_c, kt_c, tail_c = xs
        rel = masked_exp(causal, G_c[:, :, :, None, :] - G_c[:, :, None, :, :])
        A = jnp.einsum("bhrd,bhsd,bhrsd->bhrs", q_c, k_c, rel)
        o = (jnp.einsum("bhrd,bhde->bhre", qd_c, state)
             + jnp.einsum("bhrs,bhse->bhre", A, v_c))
        state = tail_c[..., None] * state + jnp.einsum("bhsd,bhse->bhde", kt_c, v_c)
        return state, o

    xs = tuple(jnp.moveaxis(t, 2, 0) for t in (q, k, v, G, q_decay, k_tail, tail))
    _, o = lax.scan(step, jnp.zeros((B, H, dk, dv), q.dtype), xs)
    return jnp.moveaxis(o, 0, 2).reshape(B, H, S, dv)


def deltanet_branch(qkv, z, b, a, conv_w, A_log, dt_bias, norm_w):
    f32 = jnp.float32
    qkv = jax.nn.silu(causal_conv(qkv.astype(f32), conv_w.astype(f32)))
    q, k, v = jnp.split(qkv, 3, axis=-1)
    q = l2_norm(to_heads(q, DN_HEADS)) * (DN_HEAD_DIM ** -0.5)
    k = l2_norm(to_heads(k, DN_HEADS))
    v = to_heads(v, DN_HEADS)
    beta = jax.nn.sigmoid(b.astype(f32)).transpose(0, 2, 1)
    g = -(jnp.exp(A_log.astype(f32))
          * jax.nn.softplus(a.astype(f32) + dt_bias.astype(f32))).transpose(0, 2, 1)
    o = chunk_gated_delta_rule(q, k, v, g, beta)
    return gated_head_norm(o, z, norm_w)


def hgrn2_branch(q, f, i, z, lb, norm_w):
    f32 = jnp.float32
    q = to_heads(jax.nn.silu(q.astype(f32)), HG_HEADS)
    fh = to_heads(f.astype(f32), HG_HEADS)
    lbh = lb.astype(f32).reshape(HG_HEADS, 1, HG_HEAD_DIM)
    log_f = jnp.log(lbh + (1.0 - lbh) * jax.nn.sigmoid(fh))
    k = (1.0 - lbh) * jax.nn.sigmoid(-fh)
    v = to_heads(i.astype(f32), HG_HEADS)
    o = chunk_hgrn2(q, k, v, log_f)
    return gated_head_norm(o, z, norm_w)


def setup_inputs(seed: int = 0) -> dict:
    key = jax.random.key(seed)
    ks = jax.random.split(key, 16)
    f32 = jnp.float32
    x = jax.random.normal(ks[0], (BATCH, SEQ, D_MODEL), f32)
    p = jax.random.normal(ks[1], (DEPTH, BATCH, SEQ, PLE_DIM), f32)
    norm_w = 1.0 + 0.02 * jax.random.normal(ks[2], (DEPTH, D_MODEL), f32)
    w_in = jax.random.normal(ks[3], (DEPTH, D_MODEL, IN_WIDTH), f32) * D_MODEL ** -0.5
    dn_conv_w = jax.random.normal(ks[4], (DEPTH, CONV_WIDTH, 3 * DN_WIDTH), f32) * CONV_WIDTH ** -0.5
    dn_A_log = jnp.log(jax.random.uniform(ks[5], (DEPTH, DN_HEADS), f32, 1.0, 16.0))
    dt = jnp.exp(jax.random.uniform(ks[6], (DEPTH, DN_HEADS), f32, np.log(1e-3), np.log(1e-1)))
    dn_dt_bias = dt + jnp.log(-jnp.expm1(-dt))
    dn_norm_w = 1.0 + 0.02 * jax.random.normal(ks[7], (DEPTH, DN_HEAD_DIM), f32)
    hg_lb_logits = 0.5 * jax.random.normal(ks[8], (DEPTH, HG_WIDTH), f32)
    hg_norm_w = 1.0 + 0.02 * jax.random.normal(ks[9], (DEPTH, HG_HEAD_DIM), f32)
    w_out = jax.random.normal(ks[10], (DEPTH, MIX_WIDTH, D_MODEL), f32) * MIX_WIDTH ** -0.5
    w_ple_up = jax.random.normal(ks[11], (DEPTH, PLE_DIM, D_MODEL), f32) * PLE_DIM ** -0.5
    w_ple_gate = jax.random.normal(ks[12], (DEPTH, D_MODEL, D_MODEL), f32) * D_MODEL ** -0.5
    final_norm_w = 1.0 + 0.02 * jax.random.normal(ks[13], (D_MODEL,), f32)
    return {"x": x, "p": p, "norm_w": norm_w, "w_in": w_in, "dn_conv_w": dn_conv_w,
            "dn_A_log": dn_A_log, "dn_dt_bias": dn_dt_bias, "dn_norm_w": dn_norm_w,
            "hg_lb_logits": hg_lb_logits, "hg_norm_w": hg_norm_w, "w_out": w_out,
            "w_ple_up": w_ple_up, "w_ple_gate": w_ple_gate, "final_norm_w": final_norm_w}


def reference(x, p, norm_w, w_in, dn_conv_w, dn_A_log, dn_dt_bias, dn_norm_w,
              hg_lb_logits, hg_norm_w, w_out, w_ple_up, w_ple_gate, final_norm_w):
    split_points = [int(s) for s in np.cumsum(SPLITS)[:-1]]
    lb_probs = jax.nn.softmax(hg_lb_logits.astype(jnp.float32), axis=0)
    lower_bounds = jnp.cumsum(lb_probs, axis=0) - lb_probs[0]
    h = x
    for l in range(DEPTH):
        hn = rms_norm(h, norm_w[l])
        proj = hn @ w_in[l]
        dn_qkv, dn_z, dn_b, dn_a, hg_q, hg_f, hg_i, hg_z = jnp.split(proj, split_points, axis=-1)
        y_a = deltanet_branch(dn_qkv, dn_z, dn_b, dn_a, dn_conv_w[l], dn_A_log[l],
                              dn_dt_bias[l], dn_norm_w[l])
        y_b = hgrn2_branch(hg_q, hg_f, hg_i, hg_z, lower_bounds[l], hg_norm_w[l])
        y = jnp.concatenate([y_a, y_b], axis=-1).astype(h.dtype)
        h = h + y @ w_out[l]
        gate = jax.nn.sigmoid((h @ w_ple_gate[l]).astype(jnp.float32))
        h = h + ((p[l] @ w_ple_up[l]).astype(jnp.float32) * gate).astype(h.dtype)
    return rms_norm(h, final_norm_w)
```

```python
import numpy as np
import concourse.bass as bass
import concourse.mybir as mybir
from concourse.bass_utils import run_bass_kernel_spmd

F32 = mybir.dt.float32
BF16 = mybir.dt.bfloat16
ALU = mybir.AluOpType
AF = mybir.ActivationFunctionType

D = 2048
S = 2048
T = 512
NT = T // 128
KC = 16
DEPTH = 2
PLE = 256
INW = 8208
NORM_EPS = 1e-6
L2_EPS = 1e-6
NEG = -30000.0

ENGS = ("pe", "act", "dve", "pool", "sp")
NDS = 16
NDW = 8
SAME_ENGINE_SYNC = True


class Tl:
    __slots__ = ("ap", "key")

    def __init__(self, ap, key):
        self.ap = ap
        self.key = key

    def __getitem__(self, idx):
        return Tl(self.ap[idx], self.key)


class Sched:
    def __init__(self):
        self.streams = {e: [] for e in ENGS}
        self.count = {e: 0 for e in ENGS}
        self.waited = {e: {} for e in ENGS}
        self.lastw = {}
        self.readers = {}
        self.dma_n = {"d": 0, "w": 0}
        self.dma_uses = {"d": [0] * NDS, "w": [0] * NDW}
        self.out_events = []

    def _collect(self, eng, reads, writes):
        evs = {}

        def add(ev):
            if ev is None:
                return
            s, v, src = ev
            if src == eng and (eng == "pe" or not SAME_ENGINE_SYNC):
                return
            if evs.get(s, 0) < v:
                evs[s] = v

        for k in reads:
            add(self.lastw.get(k))
        for k in writes:
            add(self.lastw.get(k))
            for s, (v, src) in self.readers.get(k, {}).items():
                add((s, v, src))
        out = []
        w = self.waited[eng]
        for s, v in evs.items():
            if w.get(s, 0) < v:
                w[s] = v
                out.append((s, v))
        return out

    def _record(self, ev, reads, writes):
        s, v, src = ev
        for k in reads:
            r = self.readers.setdefault(k, {})
            if r.get(s, (0, None))[0] < v:
                r[s] = (v, src)
        for k in writes:
            self.lastw[k] = ev
            self.readers[k] = {}

    @staticmethod
    def _keys(lst):
        ks = []
        for t in lst:
            k = t.key if isinstance(t, Tl) else t
            for kk in (k if isinstance(k, list) else [k]):
                if kk not in ks:
                    ks.append(kk)
        return ks

    @staticmethod
    def _is_psum(k):
        return (isinstance(k, str) and k.startswith("ps")) or (isinstance(k, tuple) and k[0] == "psA")

    def op(self, eng, fn, reads=(), writes=()):
        reads = self._keys(reads)
        writes = self._keys(writes)
        for k in reads:
            if self._is_psum(k) and k not in writes:
                writes.append(k)
        reads = [k for k in reads if not self._is_psum(k)]
        waits = self._collect(eng, reads, writes)
        self.count[eng] += 1
        ev = (eng, self.count[eng], eng)
        self.streams[eng].append(("op", fn, waits, None))
        self._record(ev, reads, writes)

    def dma(self, eng, fn, reads=(), writes=(), is_output=False):
        reads = self._keys(reads)
        writes = self._keys(writes)
        cls = "w" if eng == "pool" else "d"
        k = self.dma_n[cls] % (NDW if cls == "w" else NDS)
        self.dma_n[cls] += 1
        prev = self.dma_uses[cls][k]
        self.dma_uses[cls][k] += 1
        waits = self._collect(eng, reads, writes)
        sname = (cls, k)
        if prev > 0 and self.waited[eng].get(sname, 0) < 16 * prev:
            self.waited[eng][sname] = 16 * prev
            waits.append((sname, 16 * prev))
        ev = (sname, 16 * (prev + 1), "dma")
        self.streams[eng].append(("dma", fn, waits, sname))
        self._record(ev, reads, writes)
        if is_output:
            self.out_events.append(ev)

    def finish(self):
        waits = []
        best = {}
        for s, v, _ in self.out_events:
            best[s] = max(best.get(s, 0), v)
        for cls, n in (("d", NDS), ("w", NDW)):
            for k in range(n):
                if self.dma_uses[cls][k] > 0:
                    best[(cls, k)] = 16 * self.dma_uses[cls][k]
        for e in ENGS:
            if e != "sp" and self.count[e] > 0:
                best[e] = self.count[e]
        for s, v in best.items():
            waits.append((s, v))
        self.streams["sp"].append(("wait", None, waits, None))

    def emit(self, nc, sems):
        def replay(engname, e):
            own = sems.get(engname)
            for kind, fn, waits, sname in self.streams[engname]:
                for s, v in waits:
                    e.wait_ge(sems[s], v)
                if kind == "op":
                    fn(e).then_inc(own, 1)
                elif kind == "dma":
                    fn(e).then_inc(sems[sname], 16)

        with nc.Block() as block:
            @block.tensor
            def _(e):
                replay("pe", e)

            @block.scalar
            def _(e):
                replay("act", e)

            @block.vector
            def _(e):
                replay("dve", e)

            @block.gpsimd
            def _(e):
                replay("pool", e)

            @block.sync
            def _(e):
                replay("sp", e)


def _win_perm():
    perm = []
    for h in range(8):
        for base in (0, 1024, 2048, 3072):
            perm.extend(range(base + h * 128, base + (h + 1) * 128))
    for h in range(8):
        for base in (4112, 5136, 7184):
            perm.extend(range(base + h * 128, base + (h + 1) * 128))
    perm.extend(range(6160, 7184))
    perm.extend(range(4096, 4112))
    assert len(perm) == INW
    return np.array(perm)


OFF_DN = 0
OFF_HG = 8 * 512
OFF_HI = OFF_HG + 8 * 384
OFF_BA = OFF_HI + 1024

C_IDENT = 0
C_TRI = 128
C_MNEG_CS_STRICT = 256
C_MNEG_SC = 384
C_HGMASK = 512
C_ONES = 640
C_RESET = 768
C_EPS = 768 + 512
NCONST = 768 + 512 + 4


def _consts():
    c = np.zeros((128, NCONST), np.float32)
    i = np.arange(128)
    c[:, C_IDENT:C_IDENT + 128] = np.eye(128)
    c[:, C_TRI:C_TRI + 128] = (i[:, None] <= i[None, :])
    c[:, C_MNEG_CS_STRICT:C_MNEG_CS_STRICT + 128] = np.where(i[None, :] < i[:, None], 0.0, NEG)
    c[:, C_MNEG_SC:C_MNEG_SC + 128] = np.where(i[None, :] >= i[:, None], 0.0, NEG)
    c[:, C_HGMASK:C_HGMASK + 128] = ((i[None, :] >= i[:, None]) & ((i[None, :] // 64) == (i[:, None] // 64)))
    c[:, C_ONES:C_ONES + 128] = 1.0
    t = np.arange(512)
    c[:, C_RESET:C_RESET + 512] = (t % 64 != 0)[None, :]
    c[:, C_EPS + 0] = D * NORM_EPS
    c[:, C_EPS + 1] = L2_EPS
    c[:, C_EPS + 2] = 128.0 * NORM_EPS
    c[:, C_EPS + 3] = 1.0
    return c


class _Stop(Exception):
    pass


def build(nb=S // T, debug=False, stage=100, sub=100):
    nc = bass.Bass("TRN2", target_bir_lowering=False)
    P = Sched()
    ntok = nb * T

    def dram(name, shape, dt=F32, kind="ExternalInput"):
        return nc.dram_tensor(name, list(shape), dt, kind=kind).ap()

    x_d = dram("x", [S, D])
    p_d = dram("p", [DEPTH, S, PLE])
    win_d = dram("w_in", [DEPTH, 128, KC, INW])
    wout_d = dram("w_out", [DEPTH, 128, KC, D])
    wgate_d = dram("w_gate", [DEPTH, 128, KC, D])
    wup_d = dram("w_up", [DEPTH, 128, 2, D])
    consts_d = dram("consts", [128, NCONST])
    NSM_L = 16 + 96 + 8 + 8 + 1 + 1 + 8
    NSM = DEPTH * NSM_L + 16
    small_d = dram("small", [128, NSM])
    out_d = dram("out", [S, D], kind="ExternalOutput")
    dbg = {}

    from contextlib import ExitStack
    es = ExitStack()

    def sb(name, shape, dt=F32):
        return es.enter_context(nc.sbuf_tensor(name, list(shape), dt))

    def ps(name, shape, dt=F32):
        return es.enter_context(nc.psum_tensor(name, list(shape), dt))

    with es:
        sems = {}
        for e in ENGS:
            sems[e] = es.enter_context(nc.semaphore("s_" + e))
        for k in range(NDS):
            sems[("d", k)] = es.enter_context(nc.semaphore("d%d" % k))
        for k in range(NDW):
            sems[("w", k)] = es.enter_context(nc.semaphore("w%d" % k))

        consts = sb("consts_sb", [128, NCONST])
        small = sb("small_sb", [128, NSM])
        cbf = sb("cbf", [128, 256], BF16)
        hT_t = sb("hT", [128, KC, T])
        hnT_t = sb("hnT", [128, KC, T], BF16)
        yT_t = sb("yT", [128, KC, T], BF16)
        S32_t = sb("S32", [128, DEPTH * 16, 128])
        Sbf_t = sb("Sbf", [128, DEPTH * 16, 128], BF16)
        hist_t = sb("hist", [128, DEPTH * 24, 4])
        NWB = 2
        wbuf_t = [sb("wbuf%d" % i, [128, KC, 512], BF16) for i in range(NWB)]
        xin_t = sb("xin", [128, 1, D])
        sqb_t = sb("sqb", [128, 2, T], BF16)
        rstd_t = sb("rstd", [128, T])
        pre_t = sb("pre", [128, 3, T + 4])
        cv_t = sb("cv", [128, 3, T])
        qdT_t = sb("qdT", [128, T], BF16)
        rn_t = sb("rn", [128, T])
        ba_t = sb("ba", [128, 10, NT * 8])
        hE_t = sb("hE", [128, 4, T])
        _v3 = lambda ap: ap.rearrange("p (j d) -> p j d", d=128)
        gb4_t = _v3(hE_t[:, 0, :])
        u4_t = _v3(hE_t[:, 1, :])
        kbg4_t = _v3(hE_t[:, 2, 0:256].bitcast(BF16))
        ktk4_t = _v3(hE_t[:, 2, 256:512].bitcast(BF16))
        vb4_t = _v3(hE_t[:, 3, 0:256].bitcast(BF16))
        qkT4_t = _v3(hE_t[:, 3, 256:512].bitcast(BF16))
        vnew_t = sb("vnew", [128, 128], BF16)
        t1_t = sb("t1", [128, T])
        vhg_t = sb("vhg", [128, NT, 1024], BF16)
        Tb4_t = vhg_t[:, 1, 512:1024].rearrange("p (j d) -> p j d", d=128)
        wTb4_t = vhg_t[:, 2, 0:512].rearrange("p (j d) -> p j d", d=128)
        ktok4_t = sb("ktok4", [128, NT, 128], BF16)
        ATb4_t = sb("ATb4", [128, NT, 128], BF16)
        lbc_t = sb("lbc", [128, DEPTH, 2, 8])
        pT_t = sb("pT", [128, 2, T], BF16)
        gate_t = sb("gate", [128, T])
        wup_t = sb("wupb", [128, 8, 512], BF16)

        zs0_t = sb("zs0", [128, T])
        qkv0_t = sb("qkv0", [128, 3, T], BF16)
        vTb1_t = sb("vTb1", [128, T], BF16)
        hB0_t = sb("hB0", [128, 4, T], BF16)
        hB1_t = sb("hB1", [128, 4, T], BF16)
        tails_t = sb("tails", [128, 2, 8])
        zsP = [zs0_t, xin_t[:, 0, 512:1024]]
        rnp_t = xin_t[:, 0, 1024:1536]
        _xb = xin_t[:, 0, 1536:2048].bitcast(BF16)
        qTbP = [qkv0_t[:, 0, :], _xb[:, 0:512]]
        kTbP = [qkv0_t[:, 1, :], _xb[:, 512:1024]]
        vTbP = [qkv0_t[:, 2, :], vTb1_t]
        hBP = [hB0_t, hB1_t]

        XALL = [("xin", 0), ("zs", 1), "rnp", ("qTb", 1), ("kTb", 1)]

        psA = [ps("psA%d" % i, [128, 512]) for i in range(3)]
        psB = ps("psB", [128, 512])
        psT = psB[:].bitcast(BF16)
        psC = ps("psC", [128, 4, 128])
        psD = ps("psD", [128, 4, 128])
        psO = ps("psO", [128, 512])
        psE = ps("psE", [128, 4, 128])

        def tl(t, key):
            return Tl(t[:] if not isinstance(t, bass.AP) else t, key)

        def hT(c): return Tl(hT_t[:, c, :], ("hT", c))
        def hnT(c): return Tl(hnT_t[:, c, :], ("hnT", c))
        def yT(c): return Tl(yT_t[:, c, :], ("yT", c))
        def S32(l, h): return Tl(S32_t[:, l * 16 + h, :], ("S32", l, h))
        def Sbf(l, h): return Tl(Sbf_t[:, l * 16 + h, :], ("Sbf", l, h))
        CONST = Tl(consts[:], "consts")
        SMALL = Tl(small[:], "small")
        CBF = Tl(cbf[:], "cbf")
        ident_f = consts[:, C_IDENT:C_IDENT + 128]
        tri_f = consts[:, C_TRI:C_TRI + 128]
        ones_f = consts[:, C_ONES:C_ONES + 128]
        ident_b = cbf[:, 0:128]
        ones_b = cbf[:, 128:256]

        def sm(l, off, n=1):
            return small[:, l * NSM_L + off:l * NSM_L + off + n]
        SM_NW, SM_CONV, SM_ALOG, SM_DTB, SM_DNW, SM_HGW, SM_LB = 0, 16, 112, 120, 128, 129, 130
        fnw = small[:, DEPTH * NSM_L:DEPTH * NSM_L + 16]

        def ck(k):
            if stage == 3 and sub <= k:
                raise _Stop()

        def ck2(k):
            if stage == 6 and sub <= k:
                raise _Stop()

        def mm(out, lhsT, rhs, start, stop, reads, writes):
            P.op("pe", lambda e: e.matmul(out, lhsT, rhs, start=start, stop=stop), reads, writes)

        def tr(out, in_, ident, reads, writes):
            P.op("pe", lambda e: e.transpose(out, in_, ident), reads, writes)

        def act(out, in_, func, reads, writes, bias=None, scale=None):
            kw = {}
            if bias is not None:
                kw["bias"] = bias
            if scale is not None:
                kw["scale"] = scale
            P.op("act", lambda e: e.activation(out, in_, func, **kw), reads, writes)

        def tt(eng, out, a, b, op, reads, writes):
            P.op(eng, lambda e: e.tensor_tensor(out, a, b, op), reads, writes)

        def ts(eng, out, a, s1, s2, op0, op1, reads, writes):
            if op1 is None:
                P.op(eng, lambda e: e.tensor_scalar(out, a, s1, None, op0), reads, writes)
            else:
                P.op(eng, lambda e: e.tensor_scalar(out, a, s1, s2, op0, op1), reads, writes)

        def stt(out, a, s, b, op0, op1, reads, writes):
            P.op("dve", lambda e: e.scalar_tensor_tensor(out, a, s, b, op0, op1), reads, writes)

        def cp(eng, out, in_, reads, writes):
            if eng == "act":
                P.op("act", lambda e: e.copy(out, in_), reads, writes)
            else:
                P.op(eng, lambda e: e.tensor_copy(out, in_), reads, writes)

        def rsq(out, key, epsi):
            act(out, psB[:], AF.Ln, ["psB", CONST], [key], bias=consts[:, C_EPS + epsi:C_EPS + epsi + 1])
            act(out, out, AF.Exp, [key], [key], scale=-0.5)

        wstate = {"n": 0}

        def wload(src_ap, ncols, nk=KC):
            i = wstate["n"] % NWB
            wstate["n"] += 1
            key = ("wbuf", i)
            dst = wbuf_t[i][:, 0:nk, 0:ncols]
            P.dma("pool", lambda e: e.dma_start(out=dst, in_=src_ap), [], [key])
            return Tl(wbuf_t[i][:], key)

        P.dma("sp", lambda e: e.dma_start(out=consts[:], in_=consts_d), [], [CONST])
        P.dma("sp", lambda e: e.dma_start(out=small[:], in_=small_d), [], [SMALL])
        cp("dve", cbf[:, 0:128], ident_f, [CONST], [CBF])
        cp("dve", cbf[:, 128:256], ones_f, [CONST], [CBF])
        P.op("dve", lambda e: e.memset(S32_t[:], 0.0), [], [("S32", l, h) for l in range(DEPTH) for h in range(16)])
        P.op("dve", lambda e: e.memset(Sbf_t[:], 0.0), [], [("Sbf", l, h) for l in range(DEPTH) for h in range(16)])
        P.op("dve", lambda e: e.memset(hist_t[:], 0.0), [], ["hist"])
        drv_t = sb("drv", [128, DEPTH, 16 + 8 + 2])
        DRV = Tl(drv_t[:], "drv")
        fnws_t = sb("fnws", [128, 16])
        for l in range(DEPTH):
            ts("dve", drv_t[:, l, 0:16], sm(l, SM_NW, 16), float(np.sqrt(D)), None, ALU.mult, None, [SMALL], [DRV])
            act(drv_t[:, l, 16:24], sm(l, SM_ALOG, 8), AF.Exp, [SMALL], [DRV])
            ts("dve", drv_t[:, l, 16:24], drv_t[:, l, 16:24], -1.0, None, ALU.mult, None, [DRV], [DRV])
            ts("dve", drv_t[:, l, 24:26], sm(l, SM_DNW, 2), float(np.sqrt(128.0)), None, ALU.mult, None, [SMALL], [DRV])
        ts("dve", fnws_t[:], fnw, float(np.sqrt(D)), None, ALU.mult, None, [SMALL], ["fnws"])
        LBC = Tl(lbc_t[:], "lbc")
        lbt_t = sb("lbt", [128, 3, 8])
        LBT = Tl(lbt_t[:], "lbt")
        tt("dve", lbt_t[:, 0, :], sm(0, SM_LB, 8), sm(1, SM_LB, 8), ALU.subtract, [SMALL], [LBT])
        act(lbt_t[:, 1, :], lbt_t[:, 0, :], AF.Sigmoid, [LBT], [LBT])
        act(lbt_t[:, 2, :], lbt_t[:, 0, :], AF.Sigmoid, [LBT], [LBT], scale=-1.0)
        tt("dve", lbc_t[:, 0, 0, :], lbt_t[:, 1, :], lbt_t[:, 1, :], ALU.subtract, [LBT], [LBC])
        tt("dve", lbt_t[:, 0, :], lbt_t[:, 1, :], lbt_t[:, 2, :], ALU.add, [LBT], [LBT])
        tt("dve", lbc_t[:, 1, 0, :], lbt_t[:, 0, :], lbt_t[:, 1, :], ALU.subtract, [LBT], [LBC])
        for l in range(DEPTH):
            ts("dve", lbc_t[:, l, 1, :], lbc_t[:, l, 0, :], -1.0, 1.0, ALU.mult, ALU.add, [LBC], [LBC])

        SQB = [Tl(sqb_t[:, i, :], ("sqb", i)) for i in range(2)]
        sqn = {"n": 0}

        def sumsq_bcast(src_list, nchunks):
            for c in range(nchunks):
                s = src_list[c]
                q = SQB[sqn["n"] % 2]
                sqn["n"] += 1
                act(q.ap, s.ap, AF.Square, [s], [q])
                mm(psB[:], ones_b, q.ap, c == 0, c == nchunks - 1, [CBF, q], ["psB"])

        def rmsnorm_to(dst_fn, wcols, n_eps, dst_dtype_is_bf=True):
            sumsq_bcast([hT(c) for c in range(KC)], KC)
            rsq(rstd_t[:], "rstd", 0)
            for c in range(KC):
                d = dst_fn(c)
                stt(d.ap, hT(c).ap, wcols[:, c:c + 1], rstd_t[:], ALU.mult, ALU.mult, [hT(c), "rstd", DRV, "fnws"], [d])

        psa_n = {"n": 0}

        def next_psA():
            i = psa_n["n"] % 3
            psa_n["n"] += 1
            return Tl(psA[i][:], ("psA", i))

        def proj_fm(W, col0, rhs_fn, nk=KC):
            pt = next_psA()
            for k in range(nk):
                r = rhs_fn(k)
                mm(pt.ap, W.ap[:, k, col0:col0 + 128], r.ap, k == 0, k == nk - 1, [W, r], [pt])
            return pt

        def layer_block(l, tb):
            t0 = tb * T
            nwcols = drv_t[:, l, 0:16]
            rmsnorm_to(hnT, nwcols, D * NORM_EPS)
            if stage <= 1:
                return
            Wba = wload(win_d[l, :, :, OFF_BA:OFF_BA + 16], 16)
            Wn = wload(win_d[l, :, :, OFF_DN:OFF_DN + 512], 512)
            BA = Tl(ba_t[:], "ba")
            PSC = "psC"
            psba = psC[:].rearrange("p a b -> p (a b)")
            for j in range(NT):
                for k in range(KC):
                    mm(psba[:, j * 16:(j + 1) * 16], hnT_t[:, k, j * 128:(j + 1) * 128], Wba.ap[:, k, 0:16],
                       k == 0, k == KC - 1, [hnT(k), Wba], [PSC])
            psba3 = psba[:, 0:NT * 16].rearrange("p (j c) -> p j c", c=16)
            b3 = lambda i: ba_t[:, i, :].rearrange("p (j h) -> p j h", h=8)
            act(b3(0), psba3[:, :, 0:8], AF.Sigmoid, [PSC], [BA])
            ts("dve", ba_t[:, 9, :], ba_t[:, 0, :], -1.0, None, ALU.mult, None, [BA], [BA])
            for j in range(NT):
                tt("dve", ba_t[:, 1, j * 8:(j + 1) * 8], psba3[:, j, 8:16], sm(l, SM_DTB, 8), ALU.add, [PSC, SMALL], [BA])
            act(ba_t[:, 1, :], ba_t[:, 1, :], AF.Exp, [BA], [BA])
            act(ba_t[:, 1, :], ba_t[:, 1, :], AF.Ln, [BA, CONST], [BA], bias=consts[:, C_EPS + 3:C_EPS + 4])
            for j in range(NT):
                tt("dve", ba_t[:, 2, j * 8:(j + 1) * 8], ba_t[:, 1, j * 8:(j + 1) * 8], drv_t[:, l, 16:24], ALU.mult, [BA, DRV], [BA])
            mm(psba[:, 64:96], tri_f, ba_t[:, 2, :], True, True, [CONST, BA], [PSC])
            mm(psba[:, 96:128], ones_f, ba_t[:, 2, :], True, True, [CONST, BA], [PSC])
            cp("dve", ba_t[:, 3, :], psba[:, 64:96], [PSC], [BA])
            ts("dve", ba_t[:, 4, :], psba[:, 64:96], -1.0, None, ALU.mult, None, [PSC], [BA])
            act(ba_t[:, 5, :], psba[:, 64:96], AF.Exp, [PSC], [BA])
            tt("dve", ba_t[:, 5, :], ba_t[:, 5, :], ba_t[:, 0, :], ALU.mult, [BA], [BA])
            tt("dve", ba_t[:, 7, :], psba[:, 96:128], ba_t[:, 3, :], ALU.subtract, [PSC, BA], [BA])
            act(ba_t[:, 7, :], ba_t[:, 7, :], AF.Exp, [BA], [BA])
            act(ba_t[:, 8, :], psba[:, 96:128], AF.Exp, [PSC], [BA])

            if stage <= 2:
                return
            def dn_proj(h, W):
                par = h % 2
                PRE = Tl(pre_t[:], "pre")
                CV = Tl(cv_t[:], "cv")
                for part in range(4):
                    pt = proj_fm(W, part * 128, hnT)
                    if part < 3:
                        cp("act", pre_t[:, part, 4:4 + T], pt.ap, [pt], [("pre", part)])
                        yield
                    else:
                        act(zsP[par][:], pt.ap, AF.Silu, [pt], [("zs", par)])
                        yield
                ck(1)
                yield
                for part in range(3):
                    hv = hist_t[:, l * 24 + h * 3 + part, 1:4]
                    cp("pool", pre_t[:, part, 1:4], hv, ["hist"], [("pre", part)])
                    wc = lambda j_: sm(l, SM_CONV + (part * 8 + h) * 4 + j_, 1)
                    o = cv_t[:, part, :]
                    ts("dve", o, pre_t[:, part, 1:1 + T], wc(0), None, ALU.mult, None, [("pre", part), SMALL], [("cv", part)])
                    for j_ in range(1, 4):
                        stt(o, pre_t[:, part, 1 + j_:1 + j_ + T], wc(j_), o, ALU.mult, ALU.add, [("pre", part), SMALL, ("cv", part)], [("cv", part)])
                    cp("pool", hv, pre_t[:, part, 1 + T:4 + T], [("pre", part)], ["hist"])
                ck(2)
                yield
                act(cv_t[:, 0, :], cv_t[:, 0, :], AF.Silu, [("cv", 0)], [("cv", 0)])
                act(cv_t[:, 1, :], cv_t[:, 1, :], AF.Silu, [("cv", 1)], [("cv", 1)])
                act(vTbP[par][:], cv_t[:, 2, :], AF.Silu, [("cv", 2)], [("vTb", par)])
                ck(3)
                yield
                sumsq_bcast([Tl(cv_t[:, 0, :], ("cv", 0))], 1)
                rsq(rnp_t[:], "rnp", 1)
                stt(qTbP[par][:], cv_t[:, 0, :], float(128.0 ** -0.5), rnp_t[:], ALU.mult, ALU.mult, [("cv", 0), "rnp"], [("qTb", par)])
                sumsq_bcast([Tl(cv_t[:, 1, :], ("cv", 1))], 1)
                rsq(rnp_t[:], "rnp", 1)
                tt("dve", kTbP[par][:], cv_t[:, 1, :], rnp_t[:], ALU.mult, [("cv", 1), "rnp"], [("kTb", par)])

            def dn_core(h):
                par = h % 2
                ck(4)
                yield
                bah = lambda i: ba_t[:, i, :].rearrange("p (j h) -> p j h", h=8)[:, :, h:h + 1]
                bc = lambda i: bah(i).to_broadcast([128, NT, 128])
                v3 = lambda ap: ap.rearrange("p (j d) -> p j d", d=128)
                for j in range(NT):
                    js = slice(j * 128, (j + 1) * 128)
                    tr(psT[:, j * 128:(j + 1) * 128], kTbP[par][:, js], ident_b, [("kTb", par), CBF], ["psB"])
                    tr(psT[:, 512 + j * 128:512 + (j + 1) * 128], vTbP[par][:, js], ident_b, [("vTb", par), CBF], ["psB"])
                tt("dve", kbg4_t[:], v3(psT[:, 0:512]), bc(5), ALU.mult, ["psB", BA], ["kbg"])
                tt("dve", ktk4_t[:], v3(psT[:, 0:512]), bc(7), ALU.mult, ["psB", BA], ["ktk"])
                tt("dve", vb4_t[:], v3(psT[:, 512:1024]), bc(0), ALU.mult, ["psB", BA], ["vb"])
                ck(5)
                yield
                cp("pool", gb4_t[:], bc(2), [BA], ["gb"])
                for j in range(NT):
                    js = slice(j * 128, (j + 1) * 128)
                    mm(psC[:, j, :], kTbP[par][:, js], kTbP[par][:, js], True, True, [("kTb", par)], ["psC"])
                for j in range(NT):
                    js = slice(j * 128, (j + 1) * 128)
                    mm(psD[:, j, :], kTbP[par][:, js], qTbP[par][:, js], True, True, [("kTb", par), ("qTb", par)], ["psD"])
                for j in range(NT):
                    mm(psE[:, j, :], gb4_t[:, j, :], tri_f, True, True, ["gb", CONST], ["psE"])
                ck(6)
                yield
                mbc = lambda c0: consts[:, c0:c0 + 128].unsqueeze(1).to_broadcast([128, NT, 128])
                d0 = v3(t1_t[:])
                d1 = v3(gate_t[:])
                dec0 = v3(rn_t[:])
                dec1 = v3(rstd_t[:])
                dec2 = v3(xin_t[:, 0, 0:512])
                XIN0_ = ("xin", 0)
                stt(d0, psE[:], -1.0, mbc(C_MNEG_CS_STRICT), ALU.mult, ALU.add, ["psE", CONST], ["t1"])
                tt("dve", d1, psE[:], mbc(C_MNEG_SC), ALU.add, ["psE", CONST], ["gate"])
                act(dec2, psE[:], AF.Exp, ["psE"], [XIN0_])
                tt("dve", d0, d0, bc(3), ALU.add, ["t1", BA], ["t1"])
                tt("dve", d1, d1, bc(4), ALU.add, ["gate", BA], ["gate"])
                act(dec0, d0, AF.Exp, ["t1"], ["rn"])
                act(dec1, d1, AF.Exp, ["gate"], ["rstd"])
                tt("dve", qdT_t[:], qTbP[par][:], xin_t[:, 0, 0:512], ALU.mult, [("qTb", par), XIN0_], ["qdT"])
                tt("dve", qkT4_t[:], psD[:], dec1, ALU.mult, ["psD", "rstd"], ["qkT"])
                ck(7)
                yield
                def Mk(par_, tr_):
                    v = hBP[par_][:, 2 * tr_:2 * tr_ + 2, :].rearrange("p a b -> p (a b)").bitcast(F32)
                    return Tl(v.rearrange("p (j d) -> p j d", d=128), [("hB", par_, 2 * tr_), ("hB", par_, 2 * tr_ + 1)])
                tt("dve", d0, psC[:], dec0, ALU.mult, ["psC", "rn"], ["t1"])
                tt("dve", Mk(0, 0).ap, d0, bc(9), ALU.mult, ["t1", BA], [Mk(0, 0)])
                for j in range(NT):
                    tr(psC[:, j, :], Mk(0, 0).ap[:, j, :], ident_f, [Mk(0, 0), CONST], ["psC"])
                cp("act", Mk(0, 1).ap, psC[:], ["psC"], [Mk(0, 1)])
                T32 = vhg_t[:, 0, :].bitcast(F32).rearrange("p (j d) -> p j d", d=128)
                TK32 = [("vhg", 0, 0), ("vhg", 0, 1)]
                Tlo = vhg_t[:, 1, 0:512].rearrange("p (j d) -> p j d", d=128)
                TKLO = ("vhg", 1, 0)
                tt("dve", T32, psC[:], consts[:, C_IDENT:C_IDENT + 128].unsqueeze(1).to_broadcast([128, NT, 128]), ALU.add,
                   ["psC", CONST], TK32)
                yield
                ck(8)
                yield
                for kk in range(1, 7):
                    pa = (kk - 1) % 2
                    pb = kk % 2
                    for j in range(NT):
                        mm(psD[:, j, :], Mk(pa, 1).ap[:, j, :], Mk(pa, 0).ap[:, j, :], True, True, [Mk(pa, 0), Mk(pa, 1)], ["psD"])
                    cp("act", Mk(pb, 0).ap, psD[:], ["psD"], [Mk(pb, 0)])
                    yield
                    if kk < 6:
                        for j in range(NT):
                            mm(psC[:, j, :], Mk(pa, 0).ap[:, j, :], Mk(pa, 1).ap[:, j, :], True, True, [Mk(pa, 0), Mk(pa, 1)], ["psC"])
                        cp("act", Mk(pb, 1).ap, psC[:], ["psC"], [Mk(pb, 1)])
                    for j in range(NT):
                        mm(psE[:, j, :], Mk(pb, 0).ap[:, j, :], T32[:, j, :], True, True, [Mk(pb, 0)] + TK32, ["psE"])
                    tt("dve", T32, psE[:], T32, ALU.add, ["psE"] + TK32, TK32)
                    yield
                cp("act", Tb4_t[:], T32, TK32, [("vhg", 1, 1)])
                ck(9)
                yield
                tt("dve", Tlo, T32, Tb4_t[:], ALU.subtract, TK32 + [("vhg", 1, 1)], [TKLO])
                for j in range(NT):
                    mm(psC[:, j, :], Tb4_t[:, j, :], vb4_t[:, j, :], True, False, [("vhg", 1, 1), "vb"], ["psC"])
                    mm(psC[:, j, :], Tlo[:, j, :], vb4_t[:, j, :], False, True, [TKLO, "vb"], ["psC"])
                cp("act", u4_t[:], psC[:], ["psC"], ["u"])
                for j in range(NT):
                    mm(psD[:, j, :], kbg4_t[:, j, :], Tb4_t[:, j, :], True, False, ["kbg", ("vhg", 1, 1)], ["psD"])
                    mm(psD[:, j, :], kbg4_t[:, j, :], Tlo[:, j, :], False, True, ["kbg", TKLO], ["psD"])
                cp("dve", wTb4_t[:], psD[:], ["psD"], [("vhg", 2, 0)])
                ck(10)
                yield
                Sb = Sbf(l, h)
                Sf = S32(l, h)
                for j in range(NT):
                    js = slice(j * 128, (j + 1) * 128)
                    tailc = ba_t[:, 8, j * 8 + h:j * 8 + h + 1]
                    mm(psE[:, 0, :], wTb4_t[:, j, :], Sb.ap, True, True, [("vhg", 2, 0), Sb], ["psE"])
                    tt("dve", vnew_t[:], u4_t[:, j, :], psE[:, 0, :], ALU.subtract, ["u", "psE"], ["vnew"])
                    mm(psO[:, js], Sb.ap, qdT_t[:, js], True, False, [Sb, "qdT"], ["psO"])
                    mm(psO[:, js], vnew_t[:], qkT4_t[:, j, :], False, True, ["vnew", "qkT"], ["psO"])
                    mm(psE[:, 1, :], ktk4_t[:, j, :], vnew_t[:], True, True, ["ktk", "vnew"], ["psE"])
                    stt(Sf.ap, Sf.ap, tailc, psE[:, 1, :], ALU.mult, ALU.add, [Sf, BA, "psE"], [Sf])
                    cp("act", Sb.ap, Sf.ap, [Sf], [Sb])
                    yield
                ck(11)
                yield
                q = SQB[sqn["n"] % 2]
                sqn["n"] += 1
                act(q.ap, psO[:], AF.Square, ["psO"], [q])
                mm(psB[:], ones_b, q.ap, True, True, [CBF, q], ["psB"])
                rsq(rn_t[:], "rn", 2)
                stt(t1_t[:], psO[:], drv_t[:, l, 24:25], rn_t[:], ALU.mult, ALU.mult, ["psO", DRV, "rn"], ["t1"])
                tt("dve", yT(h).ap, t1_t[:], zsP[par][:], ALU.mult, ["t1", ("zs", par)], [yT(h)])

            def hg_values(Ws):
                for half in range(2):
                    W = use(Ws + half)
                    for j in range(NT):
                        pt = next_psA()
                        for k in range(KC):
                            mm(pt.ap, hnT_t[:, k, j * 128:(j + 1) * 128], W.ap[:, k, 0:512], k == 0, k == KC - 1, [hnT(k), W], [pt])
                        cp("act", vhg_t[:, j, half * 512:(half + 1) * 512], pt.ap, [pt], [("vhg", j, half)])
            def hg_proj(h, W):
                par = h % 2
                lbcol = lbc_t[:, l, 0, h:h + 1]
                omlcol = lbc_t[:, l, 1, h:h + 1]
                pt = proj_fm(W, 0, hnT)
                act(cv_t[:, 0, :], pt.ap, AF.Silu, [pt], [("cv", 0)])
                yield
                pt = proj_fm(W, 128, hnT)
                act(cv_t[:, 1, :], pt.ap, AF.Sigmoid, [pt], [("cv", 1)])
                yield
                pt = proj_fm(W, 256, hnT)
                act(zsP[par][:], pt.ap, AF.Silu, [pt], [("zs", par)])
                yield
                ck2(1)
                yield
                ts("dve", cv_t[:, 1, :], cv_t[:, 1, :], omlcol, lbcol, ALU.mult, ALU.add, [("cv", 1), LBC], [("cv", 1)])
                ts("dve", cv_t[:, 2, :], cv_t[:, 1, :], -1.0, 1.0, ALU.mult, ALU.add, [("cv", 1)], [("cv", 2)])
                act(cv_t[:, 1, :], cv_t[:, 1, :], AF.Ln, [("cv", 1)], [("cv", 1)])
                P.op("dve", lambda e: e.tensor_tensor_scan(pre_t[:, 0, 0:T], consts[:, C_RESET:C_RESET + 512], cv_t[:, 1, :], 0.0, ALU.mult, ALU.add),
                     [CONST, ("cv", 1)], [("pre", 0)])
                ck2(2)
                yield
                G3 = pre_t[:, 0, 0:T].rearrange("p (n c) -> p n c", c=64)
                D3 = lambda i: pre_t[:, 1 + i, 0:T].rearrange("p (n c) -> p n c", c=64)
                tt("dve", D3(0), G3, G3[:, :, 31:32].to_broadcast([128, 8, 64]), ALU.subtract, [("pre", 0)], [("pre", 1)])
                tt("dve", D3(1), G3[:, :, 63:64].to_broadcast([128, 8, 64]), G3, ALU.subtract, [("pre", 0)], [("pre", 2)])
                act(hE_t[:, 0, :], pre_t[:, 0, 0:T], AF.Exp, [("pre", 0)], ["gb"])
                act(hE_t[:, 1, :], pre_t[:, 1, 0:T], AF.Exp, [("pre", 1)], ["u"])
                act(hE_t[:, 2, :], pre_t[:, 1, 0:T], AF.Exp, [("pre", 1)], ["kbg", "ktk"], scale=-1.0)
                act(hE_t[:, 3, :], pre_t[:, 2, 0:T], AF.Exp, [("pre", 2)], ["vb", "qkT"])
                tt("dve", hBP[par][:, 0, :], cv_t[:, 0, :], hE_t[:, 0, :], ALU.mult, [("cv", 0), "gb"], [("hB", par, 0)])
                tt("dve", hBP[par][:, 1, :], cv_t[:, 0, :], hE_t[:, 1, :], ALU.mult, [("cv", 0), "u"], [("hB", par, 1)])
                tt("dve", hBP[par][:, 2, :], cv_t[:, 2, :], hE_t[:, 2, :], ALU.mult, [("cv", 2), "kbg", "ktk"], [("hB", par, 2)])
                tt("dve", hBP[par][:, 3, :], cv_t[:, 2, :], hE_t[:, 3, :], ALU.mult, [("cv", 2), "vb", "qkT"], [("hB", par, 3)])
                cp("pool", tails_t[:, par, :], hE_t[:, 0, :].rearrange("p (n c) -> p n c", c=64)[:, :, 63], ["gb"], [("tails", par)])
            def hg_core(h):
                par = h % 2
                ck2(3)
                yield
                Sb = Sbf(l, 8 + h)
                Sf = S32(l, 8 + h)
                v3 = lambda ap: ap.rearrange("p (j d) -> p j d", d=128)
                for j in range(NT):
                    js = slice(j * 128, (j + 1) * 128)
                    tr(psT[:, j * 128:(j + 1) * 128], hBP[par][:, 3, js], ident_b, [("hB", par, 3), CBF], ["psB"])
                cp("act", ktok4_t[:], v3(psT[:, 0:512]), ["psB"], ["ktok"])
                for j in range(NT):
                    js = slice(j * 128, (j + 1) * 128)
                    mm(psC[:, j, :], hBP[par][:, 2, js], hBP[par][:, 1, js], True, True, [("hB", par, 2), ("hB", par, 1)], ["psC"])
                tt("dve", ATb4_t[:], psC[:], consts[:, C_HGMASK:C_HGMASK + 128].unsqueeze(1).to_broadcast([128, NT, 128]),
                   ALU.mult, ["psC", CONST], ["ATb"])
                ck2(4)
                yield
                for n in range(8):
                    j, hf = n // 2, n % 2
                    ps_ = slice(hf * 64, (hf + 1) * 64)
                    bank, bkey = (psD, "psD") if hf == 0 else (psE, "psE")
                    mm(bank[:, j, :], ktok4_t[ps_, j, :], vhg_t[ps_, j, h * 128:(h + 1) * 128], True, True,
                       ["ktok", ("vhg", j, h // 4)], [bkey])
                ck2(5)
                yield
                for n in range(8):
                    j, hf = n // 2, n % 2
                    ns = slice(n * 64, (n + 1) * 64)
                    bank, bkey = (psD, "psD") if hf == 0 else (psE, "psE")
                    vt = vhg_t[:, j, h * 128:(h + 1) * 128]
                    mm(psO[:, ns], vt, ATb4_t[:, j, hf * 64:(hf + 1) * 64], True, False, [("vhg", j, h // 4), "ATb"], ["psO"])
                    mm(psO[:, ns], Sb.ap, hBP[par][:, 0, ns], False, True, [Sb, ("hB", par, 0)], ["psO"])
                    stt(Sf.ap, Sf.ap, tails_t[:, par, n:n + 1], bank[:, j, :], ALU.mult, ALU.add,
                        [Sf, ("tails", par), bkey], [Sf])
                    cp("act", Sb.ap, Sf.ap, [Sf], [Sb])
                    yield
                q = SQB[sqn["n"] % 2]
                sqn["n"] += 1
                act(q.ap, psO[:], AF.Square, ["psO"], [q])
                mm(psB[:], ones_b, q.ap, True, True, [CBF, q], ["psB"])
                rsq(rn_t[:], "rn", 2)
                stt(t1_t[:], psO[:], drv_t[:, l, 25:26], rn_t[:], ALU.mult, ALU.mult, ["psO", DRV, "rn"], ["t1"])
                tt("dve", yT(8 + h).ap, t1_t[:], zsP[par][:], ALU.mult, ["t1", ("zs", par)], [yT(8 + h)])


            def run(g):
                for _ in g:
                    pass

            def interleave(a, b):
                ga, gb_ = iter(a), iter(b) if b is not None else None
                da = db = False
                if gb_ is None:
                    db = True
                while not (da and db):
                    if not da:
                        try:
                            next(ga)
                        except StopIteration:
                            da = True
                    if not db:
                        try:
                            next(gb_)
                        except StopIteration:
                            db = True

            wsrc = [win_d[l, :, :, OFF_DN + h * 512:OFF_DN + (h + 1) * 512] for h in range(8)]
            wsrc.append(win_d[l, :, :, OFF_HI:OFF_HI + 512])
            wsrc.append(win_d[l, :, :, OFF_HI + 512:OFF_HI + 1024])
            wsrc += [win_d[l, :, :, OFF_HG + h * 384:OFF_HG + (h + 1) * 384] for h in range(8)]
            wsrc.append(wout_d[l, :, :, 0:512])
            Wl = {0: Wn}

            def use(i):
                if i + 1 < len(wsrc):
                    src = wsrc[i + 1]
                    Wl[i + 1] = wload(src, src.shape[-1])
                return Wl[i]

            nh = 8 if stage > 3 else 1
            run(dn_proj(0, use(0)))
            for h in range(nh):
                if h + 1 < nh:
                    interleave(dn_core(h), dn_proj(h + 1, use(h + 1)))
                else:
                    run(dn_core(h))
            if stage <= 4:
                return
            hg_values(8)
            if stage <= 5:
                return
            run(hg_proj(0, use(10)))
            nh = 8 if stage > 6 else 1
            for h in range(nh):
                if h + 1 < nh:
                    interleave(hg_core(h), hg_proj(h + 1, use(11 + h)))
                else:
                    run(hg_core(h))
            Wn = use(18) if nh == 8 else None

            if debug and l == 0 and tb == 0:
                dbg["yT"] = dram("dbg_yT", [128, KC, T], BF16, kind="ExternalOutput")
                P.dma("sp", lambda e: e.dma_start(out=dbg["yT"], in_=yT_t[:]), [yT(c) for c in range(KC)], [], is_output=True)

            if stage <= 7:
                return
            for og in range(4):
                W = Wn
                if og < 3:
                    Wn = wload(wout_d[l, :, :, (og + 1) * 512:(og + 2) * 512], 512)
                else:
                    Wn = wload(wgate_d[l, :, :, 0:512], 512)
                for oc in range(4):
                    c = og * 4 + oc
                    pt = proj_fm(W, oc * 128, yT)
                    tt("dve", hT(c).ap, hT(c).ap, pt.ap, ALU.add, [hT(c), pt], [hT(c)])
                    cp("act", hnT(c).ap, hT(c).ap, [hT(c)], [hnT(c)])
            if stage <= 8:
                return
            XIN0 = Tl(xin_t[:, 0, :], XALL)
            for j in range(NT):
                P.dma("sp", lambda e, j=j: e.dma_start(out=xin_t[:, 0, j * 256:(j + 1) * 256], in_=p_d[l, t0 + j * 128:t0 + (j + 1) * 128, :]),
                      [], [XIN0])
            for j in range(NT):
                for pc in range(2):
                    tr(psD[:, pc, :], xin_t[:, 0, j * 256 + pc * 128:j * 256 + (pc + 1) * 128], ident_f, [XIN0, CONST], ["psD"])
                    cp("act", pT_t[:, pc, j * 128:(j + 1) * 128], psD[:, pc, :], ["psD"], ["pT"])
            wup_src = wup_d[l].rearrange("p k (g c) -> p (k g) c", c=512)
            Wup = Tl(wup_t[:], "wupb")
            P.dma("pool", lambda e: e.dma_start(out=wup_t[:], in_=wup_src), [], [Wup])
            for gg in range(4):
                W = Wn
                if gg < 3:
                    Wn = wload(wgate_d[l, :, :, (gg + 1) * 512:(gg + 2) * 512], 512)
                for oc in range(4):
                    c = gg * 4 + oc
                    pg = proj_fm(W, oc * 128, hnT)
                    act(gate_t[:], pg.ap, AF.Sigmoid, [pg], ["gate"])
                    pu = next_psA()
                    for pc in range(2):
                        mm(pu.ap, Wup.ap[:, pc * 4 + gg, oc * 128:(oc + 1) * 128], pT_t[:, pc, :], pc == 0, pc == 1, [Wup, "pT"], [pu])
                    tt("dve", gate_t[:], gate_t[:], pu.ap, ALU.mult, ["gate", pu], ["gate"])
                    tt("pool", hT(c).ap, hT(c).ap, gate_t[:], ALU.add, [hT(c), "gate"], [hT(c)])

        for tb in range(nb):
            t0 = tb * T
            for j in range(NT):
                xi = Tl(xin_t[:, 0, :], XALL)
                src = x_d[t0 + j * 128:t0 + (j + 1) * 128, :]
                P.dma("sp", lambda e, src=src: e.dma_start(out=xin_t[:, 0, :], in_=src), [], [xi])
                for g in range(4):
                    bank, bkey = (psD, "psD") if g % 2 == 0 else (psC, "psC")
                    for c4 in range(4):
                        c = g * 4 + c4
                        tr(bank[:, c4, :], xin_t[:, 0, c * 128:(c + 1) * 128], ident_f, [xi, CONST], [bkey])
                    cp("act" if g % 2 else "dve", hT_t[:, g * 4:(g + 1) * 4, j * 128:(j + 1) * 128], bank[:], [bkey], [hT(g * 4 + i) for i in range(4)])
            for l in range(DEPTH if stage > 9 else (1 if stage > 0 else 0)):
                try:
                    layer_block(l, tb)
                except _Stop:
                    pass
            OUTF = lambda c: Tl(hT_t[:, c, :], ("hT", c))
            sumsq_bcast([hT(c) for c in range(KC)], KC)
            rsq(rstd_t[:], "rstd", 0)
            for c in range(KC):
                stt(hT(c).ap, hT(c).ap, fnws_t[:, c:c + 1], rstd_t[:], ALU.mult, ALU.mult, [hT(c), "rstd", "fnws"], [hT(c)])
            for j in range(NT):
                xi = Tl(xin_t[:, 0, :], XALL)
                for g in range(4):
                    bank, bkey = (psD, "psD") if g % 2 == 0 else (psC, "psC")
                    for c4 in range(4):
                        c = g * 4 + c4
                        tr(bank[:, c4, :], hT_t[:, c, j * 128:(j + 1) * 128], ident_f, [hT(c), CONST], [bkey])
                    cp("act" if g % 2 else "dve", xin_t[:, 0, g * 512:(g + 1) * 512], bank[:].rearrange("p a b -> p (a b)"), [bkey], [xi])
                dst = out_d[t0 + j * 128:t0 + (j + 1) * 128, :]
                P.dma("sp", lambda e, dst=dst: e.dma_start(out=dst, in_=xin_t[:, 0, :]), [xi], [], is_output=True)

        P.finish()
        P.emit(nc, sems)
    return nc, dbg


def _host_layout(inputs):
    perm = _win_perm()
    w_in = inputs["w_in"]
    win_r = np.ascontiguousarray(
        w_in.reshape(DEPTH, KC, 128, INW)[:, :, :, perm].transpose(0, 2, 1, 3))
    wout_r = np.ascontiguousarray(inputs["w_out"].reshape(DEPTH, KC, 128, D).transpose(0, 2, 1, 3))
    wgate_r = np.ascontiguousarray(inputs["w_ple_gate"].reshape(DEPTH, KC, 128, D).transpose(0, 2, 1, 3))
    wup_r = np.ascontiguousarray(inputs["w_ple_up"].reshape(DEPTH, 2, 128, D).transpose(0, 2, 1, 3))
    NSM_L = 16 + 96 + 8 + 8 + 1 + 1 + 8
    small = np.zeros((128, DEPTH * NSM_L + 16), np.float32)
    for l in range(DEPTH):
        o = l * NSM_L
        small[:, o:o + 16] = inputs["norm_w"][l].reshape(KC, 128).T
        cw = inputs["dn_conv_w"][l]
        small[:, o + 16:o + 112] = cw.reshape(4, 24, 128).transpose(2, 1, 0).reshape(128, 96)
        small[:, o + 112:o + 120] = np.broadcast_to(inputs["dn_A_log"][l][None, :], (128, 8))
        small[:, o + 120:o + 128] = np.broadcast_to(inputs["dn_dt_bias"][l][None, :], (128, 8))
        small[:, o + 128] = inputs["dn_norm_w"][l]
        small[:, o + 129] = inputs["hg_norm_w"][l]
        small[:, o + 130:o + 138] = inputs["hg_lb_logits"][l].reshape(8, 128).T
    small[:, DEPTH * NSM_L:] = inputs["final_norm_w"].reshape(KC, 128).T
    return win_r, wout_r, wgate_r, wup_r, small


def kernel(**inputs):
    inputs = {k: np.asarray(v) for k, v in inputs.items()}
    win_r, wout_r, wgate_r, wup_r, small = _host_layout(inputs)
    consts = _consts()
    nc, _ = build()
    B = inputs["x"].shape[0]
    in_maps = []
    for b in range(B):
        in_maps.append({
            "x": np.ascontiguousarray(inputs["x"][b]),
            "p": np.ascontiguousarray(inputs["p"][:, b]),
            "w_in": win_r, "w_out": wout_r, "w_gate": wgate_r, "w_up": wup_r,
            "consts": consts, "small": small,
        })
    res = run_bass_kernel_spmd(nc, in_maps, core_ids=list(range(B)))
    return np.stack([np.asarray(r["out"]) for r in res.results], axis=0).astype(np.float32)
```

```python
import numpy as np
import concourse.bass as bass
import concourse.mybir as mybir
from concourse.bass_utils import run_bass_kernel_spmd

F32 = mybir.dt.float32
BF16 = mybir.dt.bfloat16
ALU = mybir.AluOpType
AF = mybir.ActivationFunctionType

D = 2048
S = 2048
T = 512
NT = T // 128
KC = 16
DEPTH = 2
PLE = 256
INW = 8208
NORM_EPS = 1e-6
L2_EPS = 1e-6
NEG = -30000.0

ENGS = ("pe", "act", "dve", "pool", "sp")
NDS = 16
NDW = 8
SAME_ENGINE_SYNC = True


class Tl:
    __slots__ = ("ap", "key")

    def __init__(self, ap, key):
        self.ap = ap
        self.key = key

    def __getitem__(self, idx):
        return Tl(self.ap[idx], self.key)


class Sched:
    def __init__(self):
        self.streams = {e: [] for e in ENGS}
        self.count = {e: 0 for e in ENGS}
        self.waited = {e: {} for e in ENGS}
        self.lastw = {}
        self.readers = {}
        self.dma_n = {"d": 0, "w": 0}
        self.dma_uses = {"d": [0] * NDS, "w": [0] * NDW}
        self.out_events = []

    def _collect(self, eng, reads, writes):
        evs = {}

        def add(ev):
            if ev is None:
                return
            s, v, src = ev
            if src == eng and (eng == "pe" or not SAME_ENGINE_SYNC):
                return
            if evs.get(s, 0) < v:
                evs[s] = v

        for k in reads:
            add(self.lastw.get(k))
        for k in writes:
            add(self.lastw.get(k))
            for s, (v, src) in self.readers.get(k, {}).items():
                add((s, v, src))
        out = []
        w = self.waited[eng]
        for s, v in evs.items():
            if w.get(s, 0) < v:
                w[s] = v
                out.append((s, v))
        return out

    def _record(self, ev, reads, writes):
        s, v, src = ev
        for k in reads:
            r = self.readers.setdefault(k, {})
            if r.get(s, (0, None))[0] < v:
                r[s] = (v, src)
        for k in writes:
            self.lastw[k] = ev
            self.readers[k] = {}

    @staticmethod
    def _keys(lst):
        ks = []
        for t in lst:
            k = t.key if isinstance(t, Tl) else t
            for kk in (k if isinstance(k, list) else [k]):
                if kk not in ks:
                    ks.append(kk)
        return ks

    @staticmethod
    def _is_psum(k):
        return (isinstance(k, str) and k.startswith("ps")) or (isinstance(k, tuple) and k[0] == "psA")

    def op(self, eng, fn, reads=(), writes=()):
        reads = self._keys(reads)
        writes = self._keys(writes)
        for k in reads:
            if self._is_psum(k) and k not in writes:
                writes.append(k)
        reads = [k for k in reads if not self._is_psum(k)]
        waits = self._collect(eng, reads, writes)
        self.count[eng] += 1
        ev = (eng, self.count[eng], eng)
        self.streams[eng].append(("op", fn, waits, None))
        self._record(ev, reads, writes)

    def dma(self, eng, fn, reads=(), writes=(), is_output=False):
        reads = self._keys(reads)
        writes = self._keys(writes)
        cls = "w" if eng == "pool" else "d"
        k = self.dma_n[cls] % (NDW if cls == "w" else NDS)
        self.dma_n[cls] += 1
        prev = self.dma_uses[cls][k]
        self.dma_uses[cls][k] += 1
        waits = self._collect(eng, reads, writes)
        sname = (cls, k)
        if prev > 0 and self.waited[eng].get(sname, 0) < 16 * prev:
            self.waited[eng][sname] = 16 * prev
            waits.append((sname, 16 * prev))
        ev = (sname, 16 * (prev + 1), "dma")
        self.streams[eng].append(("dma", fn, waits, sname))
        self._record(ev, reads, writes)
        if is_output:
            self.out_events.append(ev)

    def finish(self):
        waits = []
        best = {}
        for s, v, _ in self.out_events:
            best[s] = max(best.get(s, 0), v)
        for cls, n in (("d", NDS), ("w", NDW)):
            for k in range(n):
                if self.dma_uses[cls][k] > 0:
                    best[(cls, k)] = 16 * self.dma_uses[cls][k]
        for e in ENGS:
            if e != "sp" and self.count[e] > 0:
                best[e] = self.count[e]
        for s, v in best.items():
            waits.append((s, v))
        self.streams["sp"].append(("wait", None, waits, None))

    def emit(self, nc, sems):
        def replay(engname, e):
            own = sems.get(engname)
            for kind, fn, waits, sname in self.streams[engname]:
                for s, v in waits:
                    e.wait_ge(sems[s], v)
                if kind == "op":
                    fn(e).then_inc(own, 1)
                elif kind == "dma":
                    fn(e).then_inc(sems[sname], 16)

        with nc.Block() as block:
            @block.tensor
            def _(e):
                replay("pe", e)

            @block.scalar
            def _(e):
                replay("act", e)

            @block.vector
            def _(e):
                replay("dve", e)

            @block.gpsimd
            def _(e):
                replay("pool", e)

            @block.sync
            def _(e):
                replay("sp", e)


def _win_perm():
    perm = []
    for h in range(8):
        for base in (0, 1024, 2048, 3072):
            perm.extend(range(base + h * 128, base + (h + 1) * 128))
    for h in range(8):
        for base in (4112, 5136, 7184):
            perm.extend(range(base + h * 128, base + (h + 1) * 128))
    perm.extend(range(6160, 7184))
    perm.extend(range(4096, 4112))
    assert len(perm) == INW
    return np.array(perm)


OFF_DN = 0
OFF_HG = 8 * 512
OFF_HI = OFF_HG + 8 * 384
OFF_BA = OFF_HI + 1024

C_IDENT = 0
C_TRI = 128
C_MNEG_CS_STRICT = 256
C_MNEG_SC = 384
C_HGMASK = 512
C_ONES = 640
C_RESET = 768
C_EPS = 768 + 512
NCONST = 768 + 512 + 4


def _consts():
    c = np.zeros((128, NCONST), np.float32)
    i = np.arange(128)
    c[:, C_IDENT:C_IDENT + 128] = np.eye(128)
    c[:, C_TRI:C_TRI + 128] = (i[:, None] <= i[None, :])
    c[:, C_MNEG_CS_STRICT:C_MNEG_CS_STRICT + 128] = np.where(i[None, :] < i[:, None], 0.0, NEG)
    c[:, C_MNEG_SC:C_MNEG_SC + 128] = np.where(i[None, :] >= i[:, None], 0.0, NEG)
    c[:, C_HGMASK:C_HGMASK + 128] = ((i[None, :] >= i[:, None]) & ((i[None, :] // 64) == (i[:, None] // 64)))
    c[:, C_ONES:C_ONES + 128] = 1.0
    t = np.arange(512)
    c[:, C_RESET:C_RESET + 512] = (t % 64 != 0)[None, :]
    c[:, C_EPS + 0] = D * NORM_EPS
    c[:, C_EPS + 1] = L2_EPS
    c[:, C_EPS + 2] = 128.0 * NORM_EPS
    c[:, C_EPS + 3] = 1.0
    return c


class _Stop(Exception):
    pass


def build(nb=S // T, debug=False, stage=100, sub=100):
    nc = bass.Bass("TRN2", target_bir_lowering=False)
    P = Sched()
    ntok = nb * T

    def dram(name, shape, dt=F32, kind="ExternalInput"):
        return nc.dram_tensor(name, list(shape), dt, kind=kind).ap()

    x_d = dram("x", [S, D])
    p_d = dram("p", [DEPTH, S, PLE])
    win_d = dram("w_in", [DEPTH, 128, KC, INW])
    wout_d = dram("w_out", [DEPTH, 128, KC, D])
    wgate_d = dram("w_gate", [DEPTH, 128, KC, D])
    wup_d = dram("w_up", [DEPTH, 128, 2, D])
    consts_d = dram("consts", [128, NCONST])
    NSM_L = 16 + 96 + 8 + 8 + 1 + 1 + 8
    NSM = DEPTH * NSM_L + 16
    small_d = dram("small", [128, NSM])
    out_d = dram("out", [S, D], kind="ExternalOutput")
    dbg = {}

    from contextlib import ExitStack
    es = ExitStack()

    def sb(name, shape, dt=F32):
        return es.enter_context(nc.sbuf_tensor(name, list(shape), dt))

    def ps(name, shape, dt=F32):
        return es.enter_context(nc.psum_tensor(name, list(shape), dt))

    with es:
        sems = {}
        for e in ENGS:
            sems[e] = es.enter_context(nc.semaphore("s_" + e))
        for k in range(NDS):
            sems[("d", k)] = es.enter_context(nc.semaphore("d%d" % k))
        for k in range(NDW):
            sems[("w", k)] = es.enter_context(nc.semaphore("w%d" % k))

        consts = sb("consts_sb", [128, NCONST])
        small = sb("small_sb", [128, NSM])
        cbf = sb("cbf", [128, 256], BF16)
        hT_t = sb("hT", [128, KC, T])
        hnT_t = sb("hnT", [128, KC, T], BF16)
        yT_t = sb("yT", [128, KC, T], BF16)
        S32_t = sb("S32", [128, DEPTH * 16, 128])
        Sbf_t = sb("Sbf", [128, DEPTH * 16, 128], BF16)
        hist_t = sb("hist", [128, DEPTH * 24, 4])
        NWB = 2
        wbuf_t = [sb("wbuf%d" % i, [128, KC, 512], BF16) for i in range(NWB)]
        xin_t = sb("xin", [128, 1, D])
        sqb_t = sb("sqb", [128, 2, T], BF16)
        rstd_t = sb("rstd", [128, T])
        pre_t = sb("pre", [128, 3, T + 4])
        cv_t = sb("cv", [128, 3, T])
        qdT_t = sb("qdT", [128, T], BF16)
        rn_t = sb("rn", [128, T])
        ba_t = sb("ba", [128, 10, NT * 8])
        hE_t = sb("hE", [128, 4, T])
        _v3 = lambda ap: ap.rearrange("p (j d) -> p j d", d=128)
        gb4_t = _v3(hE_t[:, 0, :])
        u4_t = _v3(hE_t[:, 1, :])
        kbg4_t = _v3(hE_t[:, 2, 0:256].bitcast(BF16))
        ktk4_t = _v3(hE_t[:, 2, 256:512].bitcast(BF16))
        vb4_t = _v3(hE_t[:, 3, 0:256].bitcast(BF16))
        qkT4_t = _v3(hE_t[:, 3, 256:512].bitcast(BF16))
        vnew_t = sb("vnew", [128, 128], BF16)
        t1_t = sb("t1", [128, T])
        vhg_t = sb("vhg", [128, NT, 1024], BF16)
        Tb4_t = vhg_t[:, 1, 512:1024].rearrange("p (j d) -> p j d", d=128)
        wTb4_t = vhg_t[:, 2, 0:512].rearrange("p (j d) -> p j d", d=128)
        ktok4_t = sb("ktok4", [128, NT, 128], BF16)
        ATb4_t = sb("ATb4", [128, NT, 128], BF16)
        lbc_t = sb("lbc", [128, DEPTH, 2, 8])
        pT_t = sb("pT", [128, 2, T], BF16)
        gate_t = sb("gate", [128, T])
        wup_t = sb("wupb", [128, 8, 512], BF16)

        zs0_t = sb("zs0", [128, T])
        qkv0_t = sb("qkv0", [128, 3, T], BF16)
        vTb1_t = sb("vTb1", [128, T], BF16)
        hB0_t = sb("hB0", [128, 4, T], BF16)
        hB1_t = sb("hB1", [128, 4, T], BF16)
        tails_t = sb("tails", [128, 2, 8])
        sbalt_t = sb("sbalt", [128, 128], BF16)
        zsP = [zs0_t, xin_t[:, 0, 512:1024]]
        rnp_t = xin_t[:, 0, 1024:1536]
        _xb = xin_t[:, 0, 1536:2048].bitcast(BF16)
        qTbP = [qkv0_t[:, 0, :], _xb[:, 0:512]]
        kTbP = [qkv0_t[:, 1, :], _xb[:, 512:1024]]
        vTbP = [qkv0_t[:, 2, :], vTb1_t]
        hBP = [hB0_t, hB1_t]

        XALL = [("xin", 0), ("zs", 1), "rnp", ("qTb", 1), ("kTb", 1)]

        psA = [ps("psA%d" % i, [128, 512]) for i in range(3)]
        psB = ps("psB", [128, 512])
        psT = psB[:].bitcast(BF16)
        psC = ps("psC", [128, 4, 128])
        psD = ps("psD", [128, 4, 128])
        psO = ps("psO", [128, 512])
        psE = ps("psE", [128, 4, 128])

        def tl(t, key):
            return Tl(t[:] if not isinstance(t, bass.AP) else t, key)

        def hT(c): return Tl(hT_t[:, c, :], ("hT", c))
        def hnT(c): return Tl(hnT_t[:, c, :], ("hnT", c))
        def yT(c): return Tl(yT_t[:, c, :], ("yT", c))
        def S32(l, h): return Tl(S32_t[:, l * 16 + h, :], ("S32", l, h))
        def Sbf(l, h): return Tl(Sbf_t[:, l * 16 + h, :], ("Sbf", l, h))
        CONST = Tl(consts[:], "consts")
        SBALT = Tl(sbalt_t[:], "sbalt")
        SMALL = Tl(small[:], "small")
        CBF = Tl(cbf[:], "cbf")
        ident_f = consts[:, C_IDENT:C_IDENT + 128]
        tri_f = consts[:, C_TRI:C_TRI + 128]
        ones_f = consts[:, C_ONES:C_ONES + 128]
        ident_b = cbf[:, 0:128]
        ones_b = cbf[:, 128:256]

        def sm(l, off, n=1):
            return small[:, l * NSM_L + off:l * NSM_L + off + n]
        SM_NW, SM_CONV, SM_ALOG, SM_DTB, SM_DNW, SM_HGW, SM_LB = 0, 16, 112, 120, 128, 129, 130
        fnw = small[:, DEPTH * NSM_L:DEPTH * NSM_L + 16]

        def ck(k):
            if stage == 3 and sub <= k:
                raise _Stop()

        def ck2(k):
            if stage == 6 and sub <= k:
                raise _Stop()

        def mm(out, lhsT, rhs, start, stop, reads, writes):
            P.op("pe", lambda e: e.matmul(out, lhsT, rhs, start=start, stop=stop), reads, writes)

        def tr(out, in_, ident, reads, writes):
            P.op("pe", lambda e: e.transpose(out, in_, ident), reads, writes)

        def act(out, in_, func, reads, writes, bias=None, scale=None):
            kw = {}
            if bias is not None:
                kw["bias"] = bias
            if scale is not None:
                kw["scale"] = scale
            P.op("act", lambda e: e.activation(out, in_, func, **kw), reads, writes)

        def tt(eng, out, a, b, op, reads, writes):
            P.op(eng, lambda e: e.tensor_tensor(out, a, b, op), reads, writes)

        def ts(eng, out, a, s1, s2, op0, op1, reads, writes):
            if op1 is None:
                P.op(eng, lambda e: e.tensor_scalar(out, a, s1, None, op0), reads, writes)
            else:
                P.op(eng, lambda e: e.tensor_scalar(out, a, s1, s2, op0, op1), reads, writes)

        def stt(out, a, s, b, op0, op1, reads, writes):
            P.op("dve", lambda e: e.scalar_tensor_tensor(out, a, s, b, op0, op1), reads, writes)

        def cp(eng, out, in_, reads, writes):
            if eng == "act":
                P.op("act", lambda e: e.copy(out, in_), reads, writes)
            else:
                P.op(eng, lambda e: e.tensor_copy(out, in_), reads, writes)

        def rsq(out, key, epsi):
            act(out, psB[:], AF.Ln, ["psB", CONST], [key], bias=consts[:, C_EPS + epsi:C_EPS + epsi + 1])
            act(out, out, AF.Exp, [key], [key], scale=-0.5)

        wstate = {"n": 0}

        def wload(src_ap, ncols, nk=KC):
            i = wstate["n"] % NWB
            wstate["n"] += 1
            key = ("wbuf", i)
            dst = wbuf_t[i][:, 0:nk, 0:ncols]
            P.dma("pool", lambda e: e.dma_start(out=dst, in_=src_ap), [], [key])
            return Tl(wbuf_t[i][:], key)

        P.dma("sp", lambda e: e.dma_start(out=consts[:], in_=consts_d), [], [CONST])
        P.dma("sp", lambda e: e.dma_start(out=small[:], in_=small_d), [], [SMALL])
        cp("dve", cbf[:, 0:128], ident_f, [CONST], [CBF])
        cp("dve", cbf[:, 128:256], ones_f, [CONST], [CBF])
        P.op("dve", lambda e: e.memset(S32_t[:], 0.0), [], [("S32", l, h) for l in range(DEPTH) for h in range(16)])
        P.op("dve", lambda e: e.memset(Sbf_t[:], 0.0), [], [("Sbf", l, h) for l in range(DEPTH) for h in range(16)])
        P.op("dve", lambda e: e.memset(hist_t[:], 0.0), [], ["hist"])
        drv_t = sb("drv", [128, DEPTH, 16 + 8 + 2])
        DRV = Tl(drv_t[:], "drv")
        fnws_t = sb("fnws", [128, 16])
        for l in range(DEPTH):
            ts("dve", drv_t[:, l, 0:16], sm(l, SM_NW, 16), float(np.sqrt(D)), None, ALU.mult, None, [SMALL], [DRV])
            act(drv_t[:, l, 16:24], sm(l, SM_ALOG, 8), AF.Exp, [SMALL], [DRV])
            ts("dve", drv_t[:, l, 16:24], drv_t[:, l, 16:24], -1.0, None, ALU.mult, None, [DRV], [DRV])
            ts("dve", drv_t[:, l, 24:26], sm(l, SM_DNW, 2), float(np.sqrt(128.0)), None, ALU.mult, None, [SMALL], [DRV])
        ts("dve", fnws_t[:], fnw, float(np.sqrt(D)), None, ALU.mult, None, [SMALL], ["fnws"])
        LBC = Tl(lbc_t[:], "lbc")
        lbt_t = sb("lbt", [128, 3, 8])
        LBT = Tl(lbt_t[:], "lbt")
        tt("dve", lbt_t[:, 0, :], sm(0, SM_LB, 8), sm(1, SM_LB, 8), ALU.subtract, [SMALL], [LBT])
        act(lbt_t[:, 1, :], lbt_t[:, 0, :], AF.Sigmoid, [LBT], [LBT])
        act(lbt_t[:, 2, :], lbt_t[:, 0, :], AF.Sigmoid, [LBT], [LBT], scale=-1.0)
        tt("dve", lbc_t[:, 0, 0, :], lbt_t[:, 1, :], lbt_t[:, 1, :], ALU.subtract, [LBT], [LBC])
        tt("dve", lbt_t[:, 0, :], lbt_t[:, 1, :], lbt_t[:, 2, :], ALU.add, [LBT], [LBT])
        tt("dve", lbc_t[:, 1, 0, :], lbt_t[:, 0, :], lbt_t[:, 1, :], ALU.subtract, [LBT], [LBC])
        for l in range(DEPTH):
            ts("dve", lbc_t[:, l, 1, :], lbc_t[:, l, 0, :], -1.0, 1.0, ALU.mult, ALU.add, [LBC], [LBC])

        SQB = [Tl(sqb_t[:, i, :], ("sqb", i)) for i in range(2)]
        sqn = {"n": 0}

        def sumsq_bcast(src_list, nchunks):
            for c in range(nchunks):
                s = src_list[c]
                q = SQB[sqn["n"] % 2]
                sqn["n"] += 1
                act(q.ap, s.ap, AF.Square, [s], [q])
                mm(psB[:], ones_b, q.ap, c == 0, c == nchunks - 1, [CBF, q], ["psB"])

        def rmsnorm_to(dst_fn, wcols, n_eps, dst_dtype_is_bf=True):
            sumsq_bcast([hT(c) for c in range(KC)], KC)
            rsq(rstd_t[:], "rstd", 0)
            for c in range(KC):
                d = dst_fn(c)
                stt(d.ap, hT(c).ap, wcols[:, c:c + 1], rstd_t[:], ALU.mult, ALU.mult, [hT(c), "rstd", DRV, "fnws"], [d])

        psa_n = {"n": 0}

        def next_psA():
            i = psa_n["n"] % 3
            psa_n["n"] += 1
            return Tl(psA[i][:], ("psA", i))

        def proj_fm(W, col0, rhs_fn, nk=KC):
            pt = next_psA()
            for k in range(nk):
                r = rhs_fn(k)
                mm(pt.ap, W.ap[:, k, col0:col0 + 128], r.ap, k == 0, k == nk - 1, [W, r], [pt])
            return pt

        def layer_block(l, tb):
            t0 = tb * T
            nwcols = drv_t[:, l, 0:16]
            rmsnorm_to(hnT, nwcols, D * NORM_EPS)
            if stage <= 1:
                return
            Wba = wload(win_d[l, :, :, OFF_BA:OFF_BA + 16], 16)
            Wn = wload(win_d[l, :, :, OFF_DN:OFF_DN + 512], 512)
            BA = Tl(ba_t[:], "ba")
            PSC = "psC"
            psba = psC[:].rearrange("p a b -> p (a b)")
            for j in range(NT):
                for k in range(KC):
                    mm(psba[:, j * 16:(j + 1) * 16], hnT_t[:, k, j * 128:(j + 1) * 128], Wba.ap[:, k, 0:16],
                       k == 0, k == KC - 1, [hnT(k), Wba], [PSC])
            psba3 = psba[:, 0:NT * 16].rearrange("p (j c) -> p j c", c=16)
            b3 = lambda i: ba_t[:, i, :].rearrange("p (j h) -> p j h", h=8)
            act(b3(0), psba3[:, :, 0:8], AF.Sigmoid, [PSC], [BA])
            ts("dve", ba_t[:, 9, :], ba_t[:, 0, :], -1.0, None, ALU.mult, None, [BA], [BA])
            for j in range(NT):
                tt("dve", ba_t[:, 1, j * 8:(j + 1) * 8], psba3[:, j, 8:16], sm(l, SM_DTB, 8), ALU.add, [PSC, SMALL], [BA])
            act(ba_t[:, 1, :], ba_t[:, 1, :], AF.Exp, [BA], [BA])
            act(ba_t[:, 1, :], ba_t[:, 1, :], AF.Ln, [BA, CONST], [BA], bias=consts[:, C_EPS + 3:C_EPS + 4])
            for j in range(NT):
                tt("dve", ba_t[:, 2, j * 8:(j + 1) * 8], ba_t[:, 1, j * 8:(j + 1) * 8], drv_t[:, l, 16:24], ALU.mult, [BA, DRV], [BA])
            mm(psba[:, 64:96], tri_f, ba_t[:, 2, :], True, True, [CONST, BA], [PSC])
            mm(psba[:, 96:128], ones_f, ba_t[:, 2, :], True, True, [CONST, BA], [PSC])
            cp("dve", ba_t[:, 3, :], psba[:, 64:96], [PSC], [BA])
            ts("dve", ba_t[:, 4, :], psba[:, 64:96], -1.0, None, ALU.mult, None, [PSC], [BA])
            act(ba_t[:, 5, :], psba[:, 64:96], AF.Exp, [PSC], [BA])
            tt("dve", ba_t[:, 5, :], ba_t[:, 5, :], ba_t[:, 0, :], ALU.mult, [BA], [BA])
            tt("dve", ba_t[:, 7, :], psba[:, 96:128], ba_t[:, 3, :], ALU.subtract, [PSC, BA], [BA])
            act(ba_t[:, 7, :], ba_t[:, 7, :], AF.Exp, [BA], [BA])
            act(ba_t[:, 8, :], psba[:, 96:128], AF.Exp, [PSC], [BA])

            if stage <= 2:
                return
            def dn_proj(h, W):
                par = h % 2
                PRE = Tl(pre_t[:], "pre")
                CV = Tl(cv_t[:], "cv")
                for part in range(4):
                    pt = proj_fm(W, part * 128, hnT)
                    if part < 3:
                        cp("act", pre_t[:, part, 4:4 + T], pt.ap, [pt], [("pre", part)])
                        yield
                    else:
                        act(zsP[par][:], pt.ap, AF.Silu, [pt], [("zs", par)])
                        yield
                ck(1)
                yield
                for part in range(3):
                    hv = hist_t[:, l * 24 + h * 3 + part, 1:4]
                    cp("pool", pre_t[:, part, 1:4], hv, ["hist"], [("pre", part)])
                    wc = lambda j_: sm(l, SM_CONV + (part * 8 + h) * 4 + j_, 1)
                    o = cv_t[:, part, :]
                    ts("dve", o, pre_t[:, part, 1:1 + T], wc(0), None, ALU.mult, None, [("pre", part), SMALL], [("cv", part)])
                    for j_ in range(1, 4):
                        stt(o, pre_t[:, part, 1 + j_:1 + j_ + T], wc(j_), o, ALU.mult, ALU.add, [("pre", part), SMALL, ("cv", part)], [("cv", part)])
                    cp("pool", hv, pre_t[:, part, 1 + T:4 + T], [("pre", part)], ["hist"])
                ck(2)
                yield
                act(cv_t[:, 0, :], cv_t[:, 0, :], AF.Silu, [("cv", 0)], [("cv", 0)])
                act(cv_t[:, 1, :], cv_t[:, 1, :], AF.Silu, [("cv", 1)], [("cv", 1)])
                act(vTbP[par][:], cv_t[:, 2, :], AF.Silu, [("cv", 2)], [("vTb", par)])
                ck(3)
                yield
                sumsq_bcast([Tl(cv_t[:, 0, :], ("cv", 0))], 1)
                rsq(rnp_t[:], "rnp", 1)
                stt(qTbP[par][:], cv_t[:, 0, :], float(128.0 ** -0.5), rnp_t[:], ALU.mult, ALU.mult, [("cv", 0), "rnp"], [("qTb", par)])
                sumsq_bcast([Tl(cv_t[:, 1, :], ("cv", 1))], 1)
                rsq(rnp_t[:], "rnp", 1)
                tt("dve", kTbP[par][:], cv_t[:, 1, :], rnp_t[:], ALU.mult, [("cv", 1), "rnp"], [("kTb", par)])

            def dn_core(h):
                par = h % 2
                ck(4)
                yield
                bah = lambda i: ba_t[:, i, :].rearrange("p (j h) -> p j h", h=8)[:, :, h:h + 1]
                bc = lambda i: bah(i).to_broadcast([128, NT, 128])
                v3 = lambda ap: ap.rearrange("p (j d) -> p j d", d=128)
                for j in range(NT):
                    js = slice(j * 128, (j + 1) * 128)
                    tr(psT[:, j * 128:(j + 1) * 128], kTbP[par][:, js], ident_b, [("kTb", par), CBF], ["psB"])
                    tr(psT[:, 512 + j * 128:512 + (j + 1) * 128], vTbP[par][:, js], ident_b, [("vTb", par), CBF], ["psB"])
                tt("dve", kbg4_t[:], v3(psT[:, 0:512]), bc(5), ALU.mult, ["psB", BA], ["kbg"])
                tt("dve", ktk4_t[:], v3(psT[:, 0:512]), bc(7), ALU.mult, ["psB", BA], ["ktk"])
                tt("dve", vb4_t[:], v3(psT[:, 512:1024]), bc(0), ALU.mult, ["psB", BA], ["vb"])
                ck(5)
                yield
                cp("pool", gb4_t[:], bc(2), [BA], ["gb"])
                for j in range(NT):
                    js = slice(j * 128, (j + 1) * 128)
                    mm(psC[:, j, :], kTbP[par][:, js], kTbP[par][:, js], True, True, [("kTb", par)], ["psC"])
                for j in range(NT):
                    js = slice(j * 128, (j + 1) * 128)
                    mm(psD[:, j, :], kTbP[par][:, js], qTbP[par][:, js], True, True, [("kTb", par), ("qTb", par)], ["psD"])
                for j in range(NT):
                    mm(psE[:, j, :], gb4_t[:, j, :], tri_f, True, True, ["gb", CONST], ["psE"])
                ck(6)
                yield
                mbc = lambda c0: consts[:, c0:c0 + 128].unsqueeze(1).to_broadcast([128, NT, 128])
                d0 = v3(t1_t[:])
                d1 = v3(gate_t[:])
                dec0 = v3(rn_t[:])
                dec1 = v3(rstd_t[:])
                dec2 = v3(xin_t[:, 0, 0:512])
                XIN0_ = ("xin", 0)
                stt(d0, psE[:], -1.0, mbc(C_MNEG_CS_STRICT), ALU.mult, ALU.add, ["psE", CONST], ["t1"])
                tt("dve", d1, psE[:], mbc(C_MNEG_SC), ALU.add, ["psE", CONST], ["gate"])
                act(dec2, psE[:], AF.Exp, ["psE"], [XIN0_])
                tt("dve", d0, d0, bc(3), ALU.add, ["t1", BA], ["t1"])
                tt("dve", d1, d1, bc(4), ALU.add, ["gate", BA], ["gate"])
                act(dec0, d0, AF.Exp, ["t1"], ["rn"])
                act(dec1, d1, AF.Exp, ["gate"], ["rstd"])
                tt("dve", qdT_t[:], qTbP[par][:], xin_t[:, 0, 0:512], ALU.mult, [("qTb", par), XIN0_], ["qdT"])
                tt("dve", qkT4_t[:], psD[:], dec1, ALU.mult, ["psD", "rstd"], ["qkT"])
                ck(7)
                yield
                def Mk(par_, tr_):
                    v = hBP[par_][:, 2 * tr_:2 * tr_ + 2, :].rearrange("p a b -> p (a b)").bitcast(F32)
                    return Tl(v.rearrange("p (j d) -> p j d", d=128), [("hB", par_, 2 * tr_), ("hB", par_, 2 * tr_ + 1)])
                tt("dve", d0, psC[:], dec0, ALU.mult, ["psC", "rn"], ["t1"])
                tt("dve", Mk(0, 0).ap, d0, bc(9), ALU.mult, ["t1", BA], [Mk(0, 0)])
                for j in range(NT):
                    tr(psC[:, j, :], Mk(0, 0).ap[:, j, :], ident_f, [Mk(0, 0), CONST], ["psC"])
                cp("act", Mk(0, 1).ap, psC[:], ["psC"], [Mk(0, 1)])
                T32 = vhg_t[:, 0, :].bitcast(F32).rearrange("p (j d) -> p j d", d=128)
                TK32 = [("vhg", 0, 0), ("vhg", 0, 1)]
                Tlo = vhg_t[:, 1, 0:512].rearrange("p (j d) -> p j d", d=128)
                TKLO = ("vhg", 1, 0)
                tt("dve", T32, psC[:], consts[:, C_IDENT:C_IDENT + 128].unsqueeze(1).to_broadcast([128, NT, 128]), ALU.add,
                   ["psC", CONST], TK32)
                yield
                ck(8)
                yield
                for kk in range(1, 7):
                    pa = (kk - 1) % 2
                    pb = kk % 2
                    for j in range(NT):
                        mm(psD[:, j, :], Mk(pa, 1).ap[:, j, :], Mk(pa, 0).ap[:, j, :], True, True, [Mk(pa, 0), Mk(pa, 1)], ["psD"])
                    cp("act", Mk(pb, 0).ap, psD[:], ["psD"], [Mk(pb, 0)])
                    yield
                    if kk < 6:
                        for j in range(NT):
                            mm(psC[:, j, :], Mk(pa, 0).ap[:, j, :], Mk(pa, 1).ap[:, j, :], True, True, [Mk(pa, 0), Mk(pa, 1)], ["psC"])
                        cp("act", Mk(pb, 1).ap, psC[:], ["psC"], [Mk(pb, 1)])
                    for j in range(NT):
                        mm(psE[:, j, :], Mk(pb, 0).ap[:, j, :], T32[:, j, :], True, True, [Mk(pb, 0)] + TK32, ["psE"])
                    tt("dve", T32, psE[:], T32, ALU.add, ["psE"] + TK32, TK32)
                    yield
                cp("act", Tb4_t[:], T32, TK32, [("vhg", 1, 1)])
                ck(9)
                yield
                tt("dve", Tlo, T32, Tb4_t[:], ALU.subtract, TK32 + [("vhg", 1, 1)], [TKLO])
                for j in range(NT):
                    mm(psC[:, j, :], Tb4_t[:, j, :], vb4_t[:, j, :], True, False, [("vhg", 1, 1), "vb"], ["psC"])
                    mm(psC[:, j, :], Tlo[:, j, :], vb4_t[:, j, :], False, True, [TKLO, "vb"], ["psC"])
                cp("act", u4_t[:], psC[:], ["psC"], ["u"])
                for j in range(NT):
                    mm(psD[:, j, :], kbg4_t[:, j, :], Tb4_t[:, j, :], True, False, ["kbg", ("vhg", 1, 1)], ["psD"])
                    mm(psD[:, j, :], kbg4_t[:, j, :], Tlo[:, j, :], False, True, ["kbg", TKLO], ["psD"])
                cp("dve", wTb4_t[:], psD[:], ["psD"], [("vhg", 2, 0)])
                ck(10)
                yield
                Sbs = [Sbf(l, h), SBALT]
                Sf = S32(l, h)
                for j in range(NT):
                    js = slice(j * 128, (j + 1) * 128)
                    tailc = ba_t[:, 8, j * 8 + h:j * 8 + h + 1]
                    Sb, Sn = Sbs[j % 2], Sbs[(j + 1) % 2]
                    mm(psE[:, 0, :], wTb4_t[:, j, :], Sb.ap, True, True, [("vhg", 2, 0), Sb], ["psE"])
                    tt("dve", vnew_t[:], u4_t[:, j, :], psE[:, 0, :], ALU.subtract, ["u", "psE"], ["vnew"])
                    mm(psO[:, js], Sb.ap, qdT_t[:, js], True, False, [Sb, "qdT"], ["psO"])
                    mm(psO[:, js], vnew_t[:], qkT4_t[:, j, :], False, True, ["vnew", "qkT"], ["psO"])
                    mm(psE[:, 1, :], ktk4_t[:, j, :], vnew_t[:], True, True, ["ktk", "vnew"], ["psE"])
                    stt(Sn.ap, Sf.ap, tailc, psE[:, 1, :], ALU.mult, ALU.add, [Sf, BA, "psE"], [Sn])
                    stt(Sf.ap, Sf.ap, tailc, psE[:, 1, :], ALU.mult, ALU.add, [Sf, BA, "psE"], [Sf])
                    yield
                ck(11)
                yield
                q = SQB[sqn["n"] % 2]
                sqn["n"] += 1
                act(q.ap, psO[:], AF.Square, ["psO"], [q])
                mm(psB[:], ones_b, q.ap, True, True, [CBF, q], ["psB"])
                rsq(rn_t[:], "rn", 2)
                stt(t1_t[:], psO[:], drv_t[:, l, 24:25], rn_t[:], ALU.mult, ALU.mult, ["psO", DRV, "rn"], ["t1"])
                tt("dve", yT(h).ap, t1_t[:], zsP[par][:], ALU.mult, ["t1", ("zs", par)], [yT(h)])

            def hg_values(Ws):
                for half in range(2):
                    W = use(Ws + half)
                    for j in range(NT):
                        pt = next_psA()
                        for k in range(KC):
                            mm(pt.ap, hnT_t[:, k, j * 128:(j + 1) * 128], W.ap[:, k, 0:512], k == 0, k == KC - 1, [hnT(k), W], [pt])
                        cp("act", vhg_t[:, j, half * 512:(half + 1) * 512], pt.ap, [pt], [("vhg", j, half)])
            def hg_proj(h, W):
                par = h % 2
                lbcol = lbc_t[:, l, 0, h:h + 1]
                omlcol = lbc_t[:, l, 1, h:h + 1]
                pt = proj_fm(W, 0, hnT)
                act(cv_t[:, 0, :], pt.ap, AF.Silu, [pt], [("cv", 0)])
                yield
                pt = proj_fm(W, 128, hnT)
                act(cv_t[:, 1, :], pt.ap, AF.Sigmoid, [pt], [("cv", 1)])
                yield
                pt = proj_fm(W, 256, hnT)
                act(zsP[par][:], pt.ap, AF.Silu, [pt], [("zs", par)])
                yield
                ck2(1)
                yield
                ts("dve", cv_t[:, 1, :], cv_t[:, 1, :], omlcol, lbcol, ALU.mult, ALU.add, [("cv", 1), LBC], [("cv", 1)])
                ts("dve", cv_t[:, 2, :], cv_t[:, 1, :], -1.0, 1.0, ALU.mult, ALU.add, [("cv", 1)], [("cv", 2)])
                act(cv_t[:, 1, :], cv_t[:, 1, :], AF.Ln, [("cv", 1)], [("cv", 1)])
                P.op("dve", lambda e: e.tensor_tensor_scan(pre_t[:, 0, 0:T], consts[:, C_RESET:C_RESET + 512], cv_t[:, 1, :], 0.0, ALU.mult, ALU.add),
                     [CONST, ("cv", 1)], [("pre", 0)])
                ck2(2)
                yield
                G3 = pre_t[:, 0, 0:T].rearrange("p (n c) -> p n c", c=64)
                D3 = lambda i: pre_t[:, 1 + i, 0:T].rearrange("p (n c) -> p n c", c=64)
                tt("dve", D3(0), G3, G3[:, :, 31:32].to_broadcast([128, 8, 64]), ALU.subtract, [("pre", 0)], [("pre", 1)])
                tt("dve", D3(1), G3[:, :, 63:64].to_broadcast([128, 8, 64]), G3, ALU.subtract, [("pre", 0)], [("pre", 2)])
                act(hE_t[:, 0, :], pre_t[:, 0, 0:T], AF.Exp, [("pre", 0)], ["gb"])
                act(hE_t[:, 1, :], pre_t[:, 1, 0:T], AF.Exp, [("pre", 1)], ["u"])
                act(hE_t[:, 2, :], pre_t[:, 1, 0:T], AF.Exp, [("pre", 1)], ["kbg", "ktk"], scale=-1.0)
                act(hE_t[:, 3, :], pre_t[:, 2, 0:T], AF.Exp, [("pre", 2)], ["vb", "qkT"])
                tt("dve", hBP[par][:, 0, :], cv_t[:, 0, :], hE_t[:, 0, :], ALU.mult, [("cv", 0), "gb"], [("hB", par, 0)])
                tt("dve", hBP[par][:, 1, :], cv_t[:, 0, :], hE_t[:, 1, :], ALU.mult, [("cv", 0), "u"], [("hB", par, 1)])
                tt("dve", hBP[par][:, 2, :], cv_t[:, 2, :], hE_t[:, 2, :], ALU.mult, [("cv", 2), "kbg", "ktk"], [("hB", par, 2)])
                tt("dve", hBP[par][:, 3, :], cv_t[:, 2, :], hE_t[:, 3, :], ALU.mult, [("cv", 2), "vb", "qkT"], [("hB", par, 3)])
                cp("pool", tails_t[:, par, :], hE_t[:, 0, :].rearrange("p (n c) -> p n c", c=64)[:, :, 63], ["gb"], [("tails", par)])
            def hg_core(h):
                par = h % 2
                ck2(3)
                yield
                Sbs = [Sbf(l, 8 + h), SBALT]
                Sf = S32(l, 8 + h)
                v3 = lambda ap: ap.rearrange("p (j d) -> p j d", d=128)
                for j in range(NT):
                    js = slice(j * 128, (j + 1) * 128)
                    tr(psT[:, j * 128:(j + 1) * 128], hBP[par][:, 3, js], ident_b, [("hB", par, 3), CBF], ["psB"])
                cp("act", ktok4_t[:], v3(psT[:, 0:512]), ["psB"], ["ktok"])
                for j in range(NT):
                    js = slice(j * 128, (j + 1) * 128)
                    mm(psC[:, j, :], hBP[par][:, 2, js], hBP[par][:, 1, js], True, True, [("hB", par, 2), ("hB", par, 1)], ["psC"])
                tt("dve", ATb4_t[:], psC[:], consts[:, C_HGMASK:C_HGMASK + 128].unsqueeze(1).to_broadcast([128, NT, 128]),
                   ALU.mult, ["psC", CONST], ["ATb"])
                ck2(4)
                yield
                for n in range(8):
                    j, hf = n // 2, n % 2
                    ps_ = slice(hf * 64, (hf + 1) * 64)
                    bank, bkey = (psD, "psD") if hf == 0 else (psE, "psE")
                    mm(bank[:, j, :], ktok4_t[ps_, j, :], vhg_t[ps_, j, h * 128:(h + 1) * 128], True, True,
                       ["ktok", ("vhg", j, h // 4)], [bkey])
                ck2(5)
                yield
                for n in range(8):
                    j, hf = n // 2, n % 2
                    ns = slice(n * 64, (n + 1) * 64)
                    bank, bkey = (psD, "psD") if hf == 0 else (psE, "psE")
                    vt = vhg_t[:, j, h * 128:(h + 1) * 128]
                    Sb, Sn = Sbs[n % 2], Sbs[(n + 1) % 2]
                    mm(psO[:, ns], vt, ATb4_t[:, j, hf * 64:(hf + 1) * 64], True, False, [("vhg", j, h // 4), "ATb"], ["psO"])
                    mm(psO[:, ns], Sb.ap, hBP[par][:, 0, ns], False, True, [Sb, ("hB", par, 0)], ["psO"])
                    stt(Sn.ap, Sf.ap, tails_t[:, par, n:n + 1], bank[:, j, :], ALU.mult, ALU.add,
                        [Sf, ("tails", par), bkey], [Sn])
                    stt(Sf.ap, Sf.ap, tails_t[:, par, n:n + 1], bank[:, j, :], ALU.mult, ALU.add,
                        [Sf, ("tails", par), bkey], [Sf])
                    yield
                q = SQB[sqn["n"] % 2]
                sqn["n"] += 1
                act(q.ap, psO[:], AF.Square, ["psO"], [q])
                mm(psB[:], ones_b, q.ap, True, True, [CBF, q], ["psB"])
                rsq(rn_t[:], "rn", 2)
                stt(t1_t[:], psO[:], drv_t[:, l, 25:26], rn_t[:], ALU.mult, ALU.mult, ["psO", DRV, "rn"], ["t1"])
                tt("dve", yT(8 + h).ap, t1_t[:], zsP[par][:], ALU.mult, ["t1", ("zs", par)], [yT(8 + h)])


            def run(g):
                for _ in g:
                    pass

            def interleave(a, b):
                ga, gb_ = iter(a), iter(b) if b is not None else None
                da = db = False
                if gb_ is None:
                    db = True
                while not (da and db):
                    if not da:
                        try:
                            next(ga)
                        except StopIteration:
                            da = True
                    if not db:
                        try:
                            next(gb_)
                        except StopIteration:
                            db = True

            wsrc = [win_d[l, :, :, OFF_DN + h * 512:OFF_DN + (h + 1) * 512] for h in range(8)]
            wsrc.append(win_d[l, :, :, OFF_HI:OFF_HI + 512])
            wsrc.append(win_d[l, :, :, OFF_HI + 512:OFF_HI + 1024])
            wsrc += [win_d[l, :, :, OFF_HG + h * 384:OFF_HG + (h + 1) * 384] for h in range(8)]
            wsrc.append(wout_d[l, :, :, 0:512])
            Wl = {0: Wn}

            def use(i):
                if i + 1 < len(wsrc):
                    src = wsrc[i + 1]
                    Wl[i + 1] = wload(src, src.shape[-1])
                return Wl[i]

            nh = 8 if stage > 3 else 1
            run(dn_proj(0, use(0)))
            for h in range(nh):
                if h + 1 < nh:
                    interleave(dn_core(h), dn_proj(h + 1, use(h + 1)))
                else:
                    run(dn_core(h))
            if stage <= 4:
                return
            hg_values(8)
            if stage <= 5:
                return
            run(hg_proj(0, use(10)))
            nh = 8 if stage > 6 else 1
            for h in range(nh):
                if h + 1 < nh:
                    interleave(hg_core(h), hg_proj(h + 1, use(11 + h)))
                else:
                    run(hg_core(h))
            Wn = use(18) if nh == 8 else None

            if debug and l == 0 and tb == 0:
                dbg["yT"] = dram("dbg_yT", [128, KC, T], BF16, kind="ExternalOutput")
                P.dma("sp", lambda e: e.dma_start(out=dbg["yT"], in_=yT_t[:]), [yT(c) for c in range(KC)], [], is_output=True)

            if stage <= 7:
                return
            for og in range(4):
                W = Wn
                if og < 3:
                    Wn = wload(wout_d[l, :, :, (og + 1) * 512:(og + 2) * 512], 512)
                else:
                    Wn = wload(wgate_d[l, :, :, 0:512], 512)
                for oc in range(4):
                    c = og * 4 + oc
                    pt = proj_fm(W, oc * 128, yT)
                    tt("dve", hT(c).ap, hT(c).ap, pt.ap, ALU.add, [hT(c), pt], [hT(c)])
                    cp("act", hnT(c).ap, hT(c).ap, [hT(c)], [hnT(c)])
            if stage <= 8:
                return
            XIN0 = Tl(xin_t[:, 0, :], XALL)
            for j in range(NT):
                P.dma("sp", lambda e, j=j: e.dma_start(out=xin_t[:, 0, j * 256:(j + 1) * 256], in_=p_d[l, t0 + j * 128:t0 + (j + 1) * 128, :]),
                      [], [XIN0])
            for j in range(NT):
                for pc in range(2):
                    tr(psD[:, pc, :], xin_t[:, 0, j * 256 + pc * 128:j * 256 + (pc + 1) * 128], ident_f, [XIN0, CONST], ["psD"])
                    cp("act", pT_t[:, pc, j * 128:(j + 1) * 128], psD[:, pc, :], ["psD"], ["pT"])
            wup_src = wup_d[l].rearrange("p k (g c) -> p (k g) c", c=512)
            Wup = Tl(wup_t[:], "wupb")
            P.dma("pool", lambda e: e.dma_start(out=wup_t[:], in_=wup_src), [], [Wup])
            for gg in range(4):
                W = Wn
                if gg < 3:
                    Wn = wload(wgate_d[l, :, :, (gg + 1) * 512:(gg + 2) * 512], 512)
                for oc in range(4):
                    c = gg * 4 + oc
                    pg = proj_fm(W, oc * 128, hnT)
                    act(gate_t[:], pg.ap, AF.Sigmoid, [pg], ["gate"])
                    pu = next_psA()
                    for pc in range(2):
                        mm(pu.ap, Wup.ap[:, pc * 4 + gg, oc * 128:(oc + 1) * 128], pT_t[:, pc, :], pc == 0, pc == 1, [Wup, "pT"], [pu])
                    tt("dve", gate_t[:], gate_t[:], pu.ap, ALU.mult, ["gate", pu], ["gate"])
                    tt("pool", hT(c).ap, hT(c).ap, gate_t[:], ALU.add, [hT(c), "gate"], [hT(c)])

        for tb in range(nb):
            t0 = tb * T
            for j in range(NT):
                xi = Tl(xin_t[:, 0, :], XALL)
                src = x_d[t0 + j * 128:t0 + (j + 1) * 128, :]
                P.dma("sp", lambda e, src=src: e.dma_start(out=xin_t[:, 0, :], in_=src), [], [xi])
                for g in range(4):
                    bank, bkey = (psD, "psD") if g % 2 == 0 else (psC, "psC")
                    for c4 in range(4):
                        c = g * 4 + c4
                        tr(bank[:, c4, :], xin_t[:, 0, c * 128:(c + 1) * 128], ident_f, [xi, CONST], [bkey])
                    cp("act" if g % 2 else "dve", hT_t[:, g * 4:(g + 1) * 4, j * 128:(j + 1) * 128], bank[:], [bkey], [hT(g * 4 + i) for i in range(4)])
            for l in range(DEPTH if stage > 9 else (1 if stage > 0 else 0)):
                try:
                    layer_block(l, tb)
                except _Stop:
                    pass
            OUTF = lambda c: Tl(hT_t[:, c, :], ("hT", c))
            sumsq_bcast([hT(c) for c in range(KC)], KC)
            rsq(rstd_t[:], "rstd", 0)
            for c in range(KC):
                stt(hT(c).ap, hT(c).ap, fnws_t[:, c:c + 1], rstd_t[:], ALU.mult, ALU.mult, [hT(c), "rstd", "fnws"], [hT(c)])
            for j in range(NT):
                xi = Tl(xin_t[:, 0, :], XALL)
                for g in range(4):
                    bank, bkey = (psD, "psD") if g % 2 == 0 else (psC, "psC")
                    for c4 in range(4):
                        c = g * 4 + c4
                        tr(bank[:, c4, :], hT_t[:, c, j * 128:(j + 1) * 128], ident_f, [hT(c), CONST], [bkey])
                    cp("act" if g % 2 else "dve", xin_t[:, 0, g * 512:(g + 1) * 512], bank[:].rearrange("p a b -> p (a b)"), [bkey], [xi])
                dst = out_d[t0 + j * 128:t0 + (j + 1) * 128, :]
                P.dma("sp", lambda e, dst=dst: e.dma_start(out=dst, in_=xin_t[:, 0, :]), [xi], [], is_output=True)

        P.finish()
        P.emit(nc, sems)
    return nc, dbg


def _host_layout(inputs):
    perm = _win_perm()
    w_in = inputs["w_in"]
    win_r = np.ascontiguousarray(
        w_in.reshape(DEPTH, KC, 128, INW)[:, :, :, perm].transpose(0, 2, 1, 3))
    wout_r = np.ascontiguousarray(inputs["w_out"].reshape(DEPTH, KC, 128, D).transpose(0, 2, 1, 3))
    wgate_r = np.ascontiguousarray(inputs["w_ple_gate"].reshape(DEPTH, KC, 128, D).transpose(0, 2, 1, 3))
    wup_r = np.ascontiguousarray(inputs["w_ple_up"].reshape(DEPTH, 2, 128, D).transpose(0, 2, 1, 3))
    NSM_L = 16 + 96 + 8 + 8 + 1 + 1 + 8
    small = np.zeros((128, DEPTH * NSM_L + 16), np.float32)
    for l in range(DEPTH):
        o = l * NSM_L
        small[:, o:o + 16] = inputs["norm_w"][l].reshape(KC, 128).T
        cw = inputs["dn_conv_w"][l]
        small[:, o + 16:o + 112] = cw.reshape(4, 24, 128).transpose(2, 1, 0).reshape(128, 96)
        small[:, o + 112:o + 120] = np.broadcast_to(inputs["dn_A_log"][l][None, :], (128, 8))
        small[:, o + 120:o + 128] = np.broadcast_to(inputs["dn_dt_bias"][l][None, :], (128, 8))
        small[:, o + 128] = inputs["dn_norm_w"][l]
        small[:, o + 129] = inputs["hg_norm_w"][l]
        small[:, o + 130:o + 138] = inputs["hg_lb_logits"][l].reshape(8, 128).T
    small[:, DEPTH * NSM_L:] = inputs["final_norm_w"].reshape(KC, 128).T
    return win_r, wout_r, wgate_r, wup_r, small


def kernel(**inputs):
    inputs = {k: np.asarray(v) for k, v in inputs.items()}
    win_r, wout_r, wgate_r, wup_r, small = _host_layout(inputs)
    consts = _consts()
    nc, _ = build()
    B = inputs["x"].shape[0]
    in_maps = []
    for b in range(B):
        in_maps.append({
            "x": np.ascontiguousarray(inputs["x"][b]),
            "p": np.ascontiguousarray(inputs["p"][:, b]),
            "w_in": win_r, "w_out": wout_r, "w_gate": wgate_r, "w_up": wup_r,
            "consts": consts, "small": small,
        })
    res = run_bass_kernel_spmd(nc, in_maps, core_ids=list(range(B)))
    return np.stack([np.asarray(r["out"]) for r in res.results], axis=0).astype(np.float32)
```
